# Optimizing a Trainium2 kernel written in Bass

```python
import math
import jax, jax.numpy as jnp
from jax import lax
import numpy as np

D_MODEL = 1024
BATCH = 8
SEQ = 4096
DEPTH = 2

N_A_LAYERS = DEPTH // 2
N_B_LAYERS = DEPTH - N_A_LAYERS
CONV_WIDTH = 31
N_HEADS = 16
N_KV_HEADS = 4
HEAD_DIM = 64
Q_PER_KV = N_HEADS // N_KV_HEADS
WINDOW = 128
BLOCK = 128
ROPE_DIM = HEAD_DIM // 4
ROPE_THETA = 500000.0
D_FF = 4 * D_MODEL
PLE_DIM = 256
DEEPNORM_ALPHA = (2 * DEPTH) ** 0.25
DEEPNORM_BETA = (8 * DEPTH) ** -0.25
LN_EPS = 1e-5

kernel_name = "yoco_conformer_swa_sink_deepnorm"


def layer_norm(x, g, b):
    xf = x.astype(jnp.float32)
    mu = jnp.mean(xf, axis=-1, keepdims=True)
    var = jnp.mean(jnp.square(xf - mu), axis=-1, keepdims=True)
    y = (xf - mu) * lax.rsqrt(var + LN_EPS)
    return (y * g.astype(jnp.float32) + b.astype(jnp.float32)).astype(x.dtype)


def rope_tables(seq_len):
    pos = jnp.arange(seq_len, dtype=jnp.float32)
    inv_freq = ROPE_THETA ** (-jnp.arange(0, ROPE_DIM, 2, dtype=jnp.float32) / ROPE_DIM)
    ang = pos[:, None] * inv_freq[None, :]
    return jnp.cos(ang)[:, None, :], jnp.sin(ang)[:, None, :]


def partial_rope(t, cos, sin):
    half = ROPE_DIM // 2
    x1 = t[..., :half].astype(jnp.float32)
    x2 = t[..., half:ROPE_DIM].astype(jnp.float32)
    rot = jnp.concatenate([x1 * cos - x2 * sin, x2 * cos + x1 * sin], axis=-1).astype(t.dtype)
    return jnp.concatenate([rot, t[..., ROPE_DIM:]], axis=-1)


def conformer_conv(x, w_in, b_in, w_dw, b_dw, ln_g, ln_b, w_out, b_out):
    h = x @ w_in + b_in
    a, gate = jnp.split(h, 2, axis=-1)
    h = a * jax.nn.sigmoid(gate)
    h = lax.conv_general_dilated(
        h, w_dw[:, None, :], window_strides=(1,), padding=[(CONV_WIDTH - 1, 0)],
        dimension_numbers=("NWC", "WIO", "NWC"), feature_group_count=D_MODEL) + b_dw
    h = jax.nn.silu(layer_norm(h, ln_g, ln_b))
    return h @ w_out + b_out


def shared_banded_kv(x, w_k, w_v, cos, sin):
    B, T, _ = x.shape
    nb = T // BLOCK
    k = partial_rope((x @ w_k).reshape(B, T, N_KV_HEADS, HEAD_DIM), cos, sin)
    v = (x @ w_v).reshape(B, T, N_KV_HEADS, HEAD_DIM)

    def band(t):
        tb = t.reshape(B, nb, BLOCK, N_KV_HEADS, HEAD_DIM)
        prev = jnp.pad(tb, ((0, 0), (1, 0), (0, 0), (0, 0), (0, 0)))[:, :-1]
        return jnp.concatenate([prev, tb], axis=2)

    return band(k), band(v)


def band_mask(nb):
    n = jnp.arange(nb)[:, None, None]
    a = jnp.arange(BLOCK)[None, :, None]
    s = jnp.arange(2 * BLOCK)[None, None, :]
    qpos = n * BLOCK + a
    kpos = (n - 1) * BLOCK + s
    rel = qpos - kpos
    return (kpos >= 0) & (rel >= 0) & (rel < WINDOW)


def swa_sink_attention(x, w_q, sinks, w_o, kk, vv, cos, sin):
    B, T, _ = x.shape
    nb = T // BLOCK
    q = partial_rope((x @ w_q).reshape(B, T, N_HEADS, HEAD_DIM), cos, sin)
    q = q.reshape(B, nb, BLOCK, N_KV_HEADS, Q_PER_KV, HEAD_DIM)
    scores = jnp.einsum("bnqkgd,bnskd->bnkgqs", q, kk,
                        preferred_element_type=jnp.float32) * (1.0 / math.sqrt(HEAD_DIM))
    mask = band_mask(nb)[None, :, None, None]
    scores = jnp.where(mask, scores, -jnp.inf)
    sink = sinks.astype(jnp.float32).reshape(1, 1, N_KV_HEADS, Q_PER_KV, 1, 1)
    lse = jnp.logaddexp(jax.nn.logsumexp(scores, axis=-1, keepdims=True), sink)
    probs = jnp.exp(scores - lse).astype(vv.dtype)
    out = jnp.einsum("bnkgqs,bnskd->bnqkgd", probs, vv)
    return out.reshape(B, T, N_HEADS * HEAD_DIM) @ w_o


def sq_relu_mlp(x, w_up, w_down):
    return jnp.square(jax.nn.relu(x @ w_up)) @ w_down


def setup_inputs(seed: int = 0) -> dict:
    key = jax.random.key(seed)
    ks = jax.random.split(key, 32)
    f32 = jnp.float32

    def nrm(k, shape, scale):
        return jax.random.normal(k, shape, f32) * scale

    def gain(k, shape):
        return 1.0 + 0.02 * jax.random.normal(k, shape, f32)

    D = D_MODEL
    HD = N_HEADS * HEAD_DIM
    KVD = N_KV_HEADS * HEAD_DIM
    return {
        "x": nrm(ks[0], (BATCH, SEQ, D), 1.0),
        "p": nrm(ks[1], (DEPTH, BATCH, SEQ, PLE_DIM), 1.0),
        "conv_w_in": nrm(ks[2], (N_A_LAYERS, D, 2 * D), D ** -0.5),
        "conv_b_in": nrm(ks[3], (N_A_LAYERS, 2 * D), 0.02),
        "conv_w_dw": nrm(ks[4], (N_A_LAYERS, CONV_WIDTH, D), CONV_WIDTH ** -0.5),
        "conv_b_dw": nrm(ks[5], (N_A_LAYERS, D), 0.02),
        "conv_ln_g": gain(ks[6], (N_A_LAYERS, D)),
        "conv_ln_b": nrm(ks[7], (N_A_LAYERS, D), 0.02),
        "conv_w_out": nrm(ks[8], (N_A_LAYERS, D, D), D ** -0.5 * DEEPNORM_BETA),
        "conv_b_out": nrm(ks[9], (N_A_LAYERS, D), 0.02),
        "kv_w_k": nrm(ks[10], (D, KVD), D ** -0.5),
        "kv_w_v": nrm(ks[11], (D, KVD), D ** -0.5),
        "attn_w_q": nrm(ks[12], (N_B_LAYERS, D, HD), D ** -0.5),
        "attn_sinks": nrm(ks[13], (N_B_LAYERS, N_HEADS), 0.5),
        "attn_w_o": nrm(ks[14], (N_B_LAYERS, HD, D), HD ** -0.5 * DEEPNORM_BETA),
        "mix_ln_g": gain(ks[15], (DEPTH, D)),
        "mix_ln_b": nrm(ks[16], (DEPTH, D), 0.02),
        "mlp_w_up": nrm(ks[17], (DEPTH, D, D_FF), D ** -0.5),
        "mlp_w_down": nrm(ks[18], (DEPTH, D_FF, D), D_FF ** -0.5 * DEEPNORM_BETA),
        "mlp_ln_g": gain(ks[19], (DEPTH, D)),
        "mlp_ln_b": nrm(ks[20], (DEPTH, D), 0.02),
        "ple_w_proj": nrm(ks[21], (DEPTH, PLE_DIM, D), PLE_DIM ** -0.5),
        "ple_w_gate": nrm(ks[22], (DEPTH, D, D), D ** -0.5),
    }


def reference(x, p, conv_w_in, conv_b_in, conv_w_dw, conv_b_dw, conv_ln_g, conv_ln_b,
              conv_w_out, conv_b_out, kv_w_k, kv_w_v, attn_w_q, attn_sinks, attn_w_o,
              mix_ln_g, mix_ln_b, mlp_w_up, mlp_w_down, mlp_ln_g, mlp_ln_b,
              ple_w_proj, ple_w_gate):
    T = x.shape[1]
    cos, sin = rope_tables(T)
    kk = vv = None
    for i in range(DEPTH):
        if i < N_A_LAYERS:
            y = conformer_conv(x, conv_w_in[i], conv_b_in[i], conv_w_dw[i], conv_b_dw[i],
                               conv_ln_g[i], conv_ln_b[i], conv_w_out[i], conv_b_out[i])
        else:
            if i == N_A_LAYERS:
                kk, vv = shared_banded_kv(x, kv_w_k, kv_w_v, cos, sin)
            j = i - N_A_LAYERS
            y = swa_sink_attention(x, attn_w_q[j], attn_sinks[j], attn_w_o[j], kk, vv, cos, sin)
        x = layer_norm(DEEPNORM_ALPHA * x + y, mix_ln_g[i], mix_ln_b[i])
        x = layer_norm(DEEPNORM_ALPHA * x + sq_relu_mlp(x, mlp_w_up[i], mlp_w_down[i]),
                       mlp_ln_g[i], mlp_ln_b[i])
        x = x + (p[i] @ ple_w_proj[i]) * jax.nn.sigmoid(x @ ple_w_gate[i])
    return x
```

```python
import numpy as np
import concourse.bass as bass
import concourse.mybir as mybir
from concourse.bass_utils import run_bass_kernel_spmd

F32 = mybir.dt.float32
BF16 = mybir.dt.bfloat16
AF = mybir.ActivationFunctionType
ALU = mybir.AluOpType
AX = mybir.AxisListType

D = 1024
T = 4096
TT = 1024
NT = 512
NSUB = 2
DC = 8
FC = 32
NHEAD = 16
ALPHA = float(4.0 ** 0.25)
EPS = 1e-5
NEG = -30000.0
ROPE_THETA = 500000.0
PERM_HEADS = [0, 4, 1, 5, 2, 6, 3, 7, 8, 12, 9, 13, 10, 14, 11, 15]

ENGS = ("pe", "act", "dve", "pool", "sp")


class Tile:
    __slots__ = ("name", "lw", "rd")

    def __init__(self, name):
        self.name = name
        self.lw = None
        self.rd = {}


class Op:
    __slots__ = ("eng", "fn", "deps", "marked", "seq", "is_dma", "chan", "dcount")

    def __init__(self, eng, fn, is_dma=False):
        self.eng = eng
        self.fn = fn
        self.deps = []
        self.marked = False
        self.seq = None
        self.is_dma = is_dma
        self.chan = None
        self.dcount = None


class Chan:
    def __init__(self, sem, name):
        self.sem = sem
        self.count = 0
        self.name = name


class Prog:
    def __init__(self, nc):
        self.nc = nc
        self.ops = []
        self.q = {e: [] for e in ENGS}
        self.sems = {}
        self._ctx = []

    def enter(self, cm):
        v = cm.__enter__()
        self._ctx.append(cm)
        return v

    def sbuf(self, name, shape, dtype):
        return self.enter(self.nc.sbuf_tensor("sb_" + name, list(shape), dtype))

    def psum(self, name, shape, dtype):
        return self.enter(self.nc.psum_tensor("pp_" + name, list(shape), dtype))

    def chan(self, name):
        s = self.enter(self.nc.semaphore("c_" + name))
        return Chan(s, name)

    def close(self):
        for cm in reversed(self._ctx):
            cm.__exit__(None, None, None)
        self._ctx = []

    def _track(self, op, reads, writes):
        deps = op.deps
        for t in reads:
            if t.lw is not None:
                deps.append(("raw", t.lw))
        for t in writes:
            if t.lw is not None:
                deps.append(("waw", t.lw))
            for r in t.rd.values():
                if isinstance(r, list):
                    for x in r:
                        deps.append(("war", x))
                else:
                    deps.append(("war", r))
        for t in reads:
            if op.is_dma:
                t.rd.setdefault("dma", []).append(op)
            else:
                t.rd[op.eng] = op
        for t in writes:
            t.lw = op
            t.rd = {}

    def op(self, eng, fn, reads=(), writes=()):
        o = Op(eng, fn)
        self._track(o, reads, writes)
        self.ops.append(o)
        self.q[eng].append(o)
        return o

    def dma(self, eng, chan, out, in_, reads=(), writes=()):
        def fn(e, out=out, in_=in_):
            return e.dma_start(out=out, in_=in_)
        o = Op(eng, fn, is_dma=True)
        o.chan = chan
        chan.count += 16
        o.dcount = chan.count
        self._track(o, reads, writes)
        self.ops.append(o)
        self.q[eng].append(o)
        return o

    def finalize(self, final_waits=()):
        nc = self.nc
        for e in ENGS:
            self.sems[e] = self.enter(nc.semaphore("p_" + e))
        for o in self.ops:
            real = []
            for kind, d in o.deps:
                if d.is_dma:
                    real.append(d)
                    continue
                if d.eng == o.eng and not o.is_dma:
                    if d.eng == "pe":
                        continue
                    if kind != "raw":
                        continue
                d.marked = True
                real.append(d)
            o.deps = real
        cnt = {e: 0 for e in ENGS}
        for e in ENGS:
            for o in self.q[e]:
                if o.marked and not o.is_dma:
                    cnt[e] += 1
                    o.seq = cnt[e]
        known = {e: {} for e in ENGS}
        emit = {e: [] for e in ENGS}
        nw = 0
        for o in self.ops:
            need = {}
            for d in o.deps:
                if d.is_dma:
                    key = ("c", id(d.chan))
                    sem, tgt = d.chan.sem, d.dcount
                else:
                    key = ("e", d.eng)
                    sem, tgt = self.sems[d.eng], d.seq
                if known[o.eng].get(key, 0) >= tgt:
                    continue
                if key not in need or need[key][1] < tgt:
                    need[key] = (sem, tgt)
            for key, (sem, tgt) in need.items():
                known[o.eng][key] = tgt
                emit[o.eng].append(("wait", sem, tgt))
                nw += 1
            emit[o.eng].append(("op", o))
        self.nwaits = nw
        self.counts = cnt
        sems = self.sems

        def run(engobj, ekey):
            for it in emit[ekey]:
                if it[0] == "wait":
                    engobj.wait_ge(it[1], it[2])
                else:
                    o = it[1]
                    ins = o.fn(engobj)
                    if o.is_dma:
                        ins.then_inc(o.chan.sem, 16)
                    elif o.marked:
                        ins.then_inc(sems[ekey], 1)
            if ekey == "sp":
                for ch in final_waits:
                    engobj.wait_ge(ch.sem, ch.count)

        with nc.Block() as block:
            @block.tensor
            def _(e):
                run(e, "pe")

            @block.scalar
            def _(e):
                run(e, "act")

            @block.vector
            def _(e):
                run(e, "dve")

            @block.gpsimd
            def _(e):
                run(e, "pool")

            @block.sync
            def _(e):
                run(e, "sp")


VEC_NAMES = ["b_in_a", "b_in_g", "b_dw", "cln_g", "cln_b", "b_out",
             "mix_g0", "mix_b0", "mlp_g0", "mlp_b0", "mix_g1", "mix_b1", "mlp_g1", "mlp_b1"]
VOFF = {n: i * 8 for i, n in enumerate(VEC_NAMES)}
V_WDW = len(VEC_NAMES) * 8
V_SINK = V_WDW + 8 * 31
V_EPS = V_SINK + 16
NV = V_EPS + 1


def _unit_table(layers):
    units = []
    idx = {}

    def add(key, kind, arg):
        idx[key] = len(units)
        units.append((key, kind, arg))

    if 0 in layers:
        for j in range(4):
            add(("win_a", j), "A", ("w_in", None, j * 256))
            add(("win_g", j), "A", ("w_in", None, 1024 + j * 256))
        for j in range(4):
            add(("cout", j), "A", ("w_cout", None, j * 256))
    if 1 in layers:
        for i, nm in enumerate(("k", "kr", "v")):
            add((nm, 0), "A", ("w_kk", i, 0))
        for j in range(4):
            add(("q", j), "A", ("w_qq", 0, j * 256))
            add(("qr", j), "A", ("w_qq", 1, j * 256))
        for j in range(4):
            add(("o", j), "A", ("w_o", None, j * 256))
    for l in layers:
        for j in range(16):
            add(("up", l, j), "A", ("w_up", l, j * 256))
        for m in range(8):
            for h in range(2):
                add(("down", l, m, h), "DN", ("w_down", l, m, h))
        for j in range(4):
            add(("gate", l, j), "A", ("w_gate", l, j * 256))
        add(("proj", l), "PJ", ("w_proj", l))
    return units, idx


def _layer_order(idx, l):
    o = []
    if l == 0:
        for j in range(4):
            o += [idx[("win_a", j)], idx[("win_g", j)]]
        o += [idx[("cout", j)] for j in range(4)]
    else:
        o += [idx[("k", 0)], idx[("kr", 0)], idx[("v", 0)]]
        for j in range(4):
            o += [idx[("q", j)], idx[("qr", j)]]
        o += [idx[("o", j)] for j in range(4)]
    o += [idx[("up", l, j)] for j in range(16)]
    for m in range(8):
        o += [idx[("down", l, m, 0)], idx[("down", l, m, 1)]]
    o.append(idx[("proj", l)])
    o += [idx[("gate", l, j)] for j in range(4)]
    return o


class Slot:
    __slots__ = ("ap", "tile", "idx")

    def __init__(self, ap, tile, idx):
        self.ap = ap
        self.tile = tile
        self.idx = idx

    def v3(self, kc):
        return self.ap[:, :].rearrange("p (k m) -> p k m", k=kc)


def build(layers=(0, 1), nmt=4, tlen=T):
    nc = bass.Bass("TRN2", target_bir_lowering=False)
    P = Prog(nc)
    L0 = 0 in layers
    L1 = 1 in layers
    last_layer = layers[-1]

    def din(name, shape):
        return nc.dram_tensor(name, list(shape), F32, kind="ExternalInput").ap()

    xT = din("xT", [D, tlen])
    pT = din("pT", [2, 256, tlen])
    W = {}
    if L0:
        W["w_in"] = din("w_in", [D, 2048])
        W["w_cout"] = din("w_cout", [D, D])
    if L1:
        W["w_kk"] = din("w_kk", [3, D, 256])
        W["w_qq"] = din("w_qq", [2, D, D])
        W["w_o"] = din("w_o", [D, D])
        cs_d = din("cs", [2, 128, tlen])
        msk_d = din("msk", [2, 128, 256])
    W["w_up"] = din("w_up", [2, D, 4096])
    W["w_down"] = din("w_down", [2, 4096, D])
    W["w_proj"] = din("w_proj", [2, 256, D])
    W["w_gate"] = din("w_gate", [2, D, D])
    vecs_d = din("vecs", [128, NV])
    ident_d = din("ident", [128, 128])
    outT = nc.dram_tensor("outT", [D, tlen], F32, kind="ExternalOutput").ap()

    units, uidx = _unit_table(layers)
    NU = len(units)
    wsc = nc.dram_tensor("wsc", [NU, 128, 2048], BF16).ap()
    t_wsc = [Tile(f"wsc{u}") for u in range(NU)]

    def unit_src(u):
        key, kind, arg = units[u]
        if kind == "A":
            nm, li, c0 = arg
            w = W[nm] if li is None else W[nm][li]
            return w.rearrange("(k p) n -> p k n", p=128)[:, :, c0:c0 + 256], 8
        if kind == "DN":
            nm, li, m, h = arg
            w = W[nm][li]
            return w.rearrange("(k p) n -> p k n", p=128)[:, h * 16:(h + 1) * 16, m * 128:(m + 1) * 128], 16
        nm, li = arg
        w = W[nm][li]
        return w.rearrange("(k p) n -> p k n", p=128)[:, :, :], 2

    xres = P.sbuf("xres", [128, DC, TT], F32)
    xb = P.sbuf("xb", [128, DC, TT], BF16)
    HB = 66560
    Hs = P.sbuf("Hs", [128, HB // 2], BF16)
    NG = HB // 1024
    tH = [Tile(f"H{g}") for g in range(NG)]

    def hg(lo, hi):
        return tH[lo // 1024:(hi + 1023) // 1024]

    def hv_bf(lo, n):
        return Hs[:, lo // 2:lo // 2 + n]

    def hv_f32(lo, n):
        return Hs[:, lo // 2:lo // 2 + 2 * n].bitcast(F32)

    vecs = P.sbuf("vecs", [128, NV], F32)
    identf = P.sbuf("identf", [128, 128], F32)
    identb = P.sbuf("identb", [128, 128], BF16)
    onesD = P.sbuf("onesD", [128, 128], BF16)
    t_const = Tile("const")
    tmpA = [P.sbuf(f"tmpA{i}", [128, NT], F32) for i in range(4)]
    t_tmpA = [Tile(f"tmpA{i}") for i in range(4)]
    tmpB = [P.sbuf(f"tmpB{i}", [128, NT], F32) for i in range(2)]
    t_tmpB = [Tile(f"tmpB{i}") for i in range(2)]
    zbb = [P.sbuf(f"zb{i}", [128, NT], BF16) for i in range(2)]
    zsb = [P.sbuf(f"zs{i}", [128, NT], BF16) for i in range(2)]
    t_zbb = [Tile(f"zb{i}") for i in range(2)]
    t_zsb = [Tile(f"zs{i}") for i in range(2)]
    mean_sb = [P.sbuf(f"mean{i}", [128, NT], F32) for i in range(2)]
    rstd_sb = [P.sbuf(f"rstd{i}", [128, NT], F32) for i in range(2)]
    t_mean = [Tile(f"mean{i}") for i in range(2)]
    t_rstd = [Tile(f"rstd{i}") for i in range(2)]
    pst = P.sbuf("pst", [128, 2, NT], F32)
    t_pst = Tile("pst")
    pb16 = P.sbuf("pb16", [128, 2, TT], BF16)
    t_pb16 = [Tile(f"pb16_{n}") for n in range(NSUB)]
    if L0:
        diag = [P.sbuf(f"diag{i}", [128, 31, 128], BF16) for i in range(2)]
        t_diag = [Tile(f"diag{i}") for i in range(2)]
        halo = P.sbuf("halo", [128, DC, 32], BF16)
        t_halo = [Tile(f"halo{c}") for c in range(DC)]
    if L1:
        kTz = [P.sbuf(f"kTz{g}", [128, 128 + TT], BF16) for g in range(4)]
        t_kTz = [[Tile(f"kTz{g}_{b}") for b in range(9)] for g in range(4)]
        v_sb = P.sbuf("v_sb", [128, 9, 256], BF16)
        t_v = [Tile(f"v{b}") for b in range(9)]
        msk = P.sbuf("msk", [128, 2, 256], F32)
        stt = [P.sbuf(f"stt{i}", [128, 32], F32) for i in range(2)]
        t_stt = [Tile(f"stt{i}") for i in range(2)]

    t_xres = [[Tile(f"xres{c}_{n}") for n in range(NSUB)] for c in range(DC)]
    t_xb = [[Tile(f"xb{c}_{n}") for n in range(NSUB)] for c in range(DC)]

    pst_ = [P.psum(f"ps{i}", [128, 1024], F32) for i in range(4)]
    t_bank = [Tile(f"bank{i}") for i in range(8)]

    def bank_ap(i):
        return pst_[i // 2][:, (i % 2) * 512:(i % 2) * 512 + 512]

    gbi = [0]

    def next_bank():
        i = gbi[0] % 4
        gbi[0] += 1
        return bank_ap(i), t_bank[i]

    def OP(eng, meth, reads=(), writes=(), **kw):
        return P.op(eng, lambda e: getattr(e, meth)(**kw), reads=reads, writes=writes)

    def vcol(name, c):
        o = VOFF[name] + c
        return vecs[:, o:o + 1]

    ch_x = P.chan("x")
    ch_out = [P.chan(f"out{n}") for n in range(NSUB)]
    ch_c = P.chan("consts")
    ch_p = P.chan("p")
    ch_cs = P.chan("cs")

    P.dma("sp", ch_c, vecs[:], vecs_d[:, :], writes=[t_const])
    P.dma("sp", ch_c, identf[:], ident_d[:, :], writes=[t_const])
    if L1:
        P.dma("sp", ch_c, msk[:], msk_d.rearrange("a p s -> p a s"), writes=[t_const])
    OP("dve", "tensor_copy", reads=[t_const], writes=[t_const], out=identb[:], in_=identf[:])
    OP("dve", "memset", writes=[t_const], ap=onesD[:], constant=1.0 / D)

    bars = {e: Tile("bar_" + e) for e in ENGS}
    barl = list(bars.values())
    NSTG = 4
    stg_f = [hv_f32(i * 8192, 2048) for i in range(NSTG)]
    stg_b = [hv_bf(32768 + i * 4096, 2048) for i in range(NSTG)]
    t_sf = [Tile(f"sf{i}") for i in range(NSTG)]
    t_sb = [Tile(f"sb{i}") for i in range(NSTG)]
    ch_sf = [P.chan(f"sf{i}") for i in range(NSTG)]
    ch_sb = [P.chan(f"sb{i}") for i in range(NSTG)]
    order1 = []
    for l in layers:
        order1 += _layer_order(uidx, l)
    assert sorted(order1) == list(range(NU))
    cast_eng = ("dve", "act", "pool")
    for i, u in enumerate(order1):
        s = i % NSTG
        src, kc = unit_src(u)
        P.dma("sp", ch_sf[s], stg_f[s].rearrange("p (k m) -> p k m", k=kc), src, reads=barl, writes=[t_sf[s]])
        ce = cast_eng[i % 3]
        if ce == "act":
            OP("act", "copy", reads=[t_sf[s]] + barl, writes=[t_sb[s]], out=stg_b[s], in_=stg_f[s])
        else:
            OP(ce, "tensor_copy", reads=[t_sf[s]] + barl, writes=[t_sb[s]], out=stg_b[s], in_=stg_f[s])
        P.dma("act", ch_sb[s], wsc[u, :, :], stg_b[s], reads=[t_sb[s]] + barl, writes=[t_wsc[u]])
    for e in ENGS:
        P.op(e, lambda en: en.nop(), writes=[bars[e]])

    NSLOT = 7
    ws_slots = [P.sbuf(f"ws{i}", [128, 2048], BF16) for i in range(NSLOT)]
    ws_tiles = [Tile(f"ws{i}") for i in range(NSLOT)]
    ws_ch = [P.chan(f"ws{i}") for i in range(NSLOT)]
    order = order1 * nmt
    st = {"issued": 0, "pos": 0, "rel": [True] * NSLOT}

    def pump():
        while st["issued"] < len(order):
            s = st["issued"] % NSLOT
            if not st["rel"][s]:
                break
            u = order[st["issued"]]
            P.dma("sp", ws_ch[s], ws_slots[s][:], wsc[u, :, :], reads=[t_wsc[u]], writes=[ws_tiles[s]])
            st["rel"][s] = False
            st["issued"] += 1

    def use(key):
        u = uidx[key]
        assert order[st["pos"]] == u, (key, st["pos"])
        pump()
        assert st["issued"] > st["pos"]
        s = st["pos"] % NSLOT
        st["pos"] += 1
        return Slot(ws_slots[s], ws_tiles[s], s)

    def done(*slots):
        for sl in slots:
            st["rel"][sl.idx] = True
        pump()

    def ns(n):
        return slice(n * NT, (n + 1) * NT)

    tAi = [0]
    tBi = [0]
    zi = [0]

    def tA():
        i = tAi[0] % 4
        tAi[0] += 1
        return tmpA[i], t_tmpA[i]

    def tB():
        i = tBi[0] % 2
        tBi[0] += 1
        return tmpB[i], t_tmpB[i]

    def mm_group(out_ap, out_tile, items):
        nI = len(items)
        for i, (l, r, rd) in enumerate(items):
            OP("pe", "matmul", reads=rd, writes=[out_tile], out=out_ap, lhsT=l, rhs=r, start=(i == 0), stop=(i == nI - 1))

    def stat_banks(n):
        return (bank_ap(4 + 2 * n), t_bank[4 + 2 * n]), (bank_ap(5 + 2 * n), t_bank[5 + 2 * n])

    def ln_accum(z_ap, z_tiles, c, n):
        (mb, tmb), (qb_, tqb) = stat_banks(n)
        i = zi[0] % 2
        zi[0] += 1
        OP("act", "copy", reads=z_tiles, writes=[t_zbb[i]], out=zbb[i][:], in_=z_ap)
        OP("act", "activation", reads=z_tiles, writes=[t_zsb[i]], out=zsb[i][:], in_=z_ap, func=AF.Square)
        OP("pe", "matmul", reads=[t_zbb[i], t_const], writes=[tmb], out=mb, lhsT=onesD[:], rhs=zbb[i][:], start=(c == 0), stop=(c == DC - 1))
        OP("pe", "matmul", reads=[t_zsb[i], t_const], writes=[tqb], out=qb_, lhsT=onesD[:], rhs=zsb[i][:], start=(c == 0), stop=(c == DC - 1))

    def ln_finish(n, z_aps, z_tiles, emit_out):
        (mb, tmb), (qb_, tqb) = stat_banks(n)
        OP("act", "copy", reads=[tmb], writes=[t_mean[n]], out=mean_sb[n][:], in_=mb)
        OP("dve", "tensor_tensor", reads=[t_mean[n]], writes=[t_rstd[n]], out=rstd_sb[n][:], in0=mean_sb[n][:], in1=mean_sb[n][:], op=ALU.mult)
        OP("dve", "tensor_tensor", reads=[tqb, t_rstd[n]], writes=[t_rstd[n]], out=rstd_sb[n][:], in0=qb_, in1=rstd_sb[n][:], op=ALU.subtract)
        OP("act", "activation", reads=[t_rstd[n], t_const], writes=[t_rstd[n]], out=rstd_sb[n][:], in_=rstd_sb[n][:], func=AF.Sqrt,
           bias=vecs[:, V_EPS:V_EPS + 1], scale=1.0)
        OP("dve", "reciprocal", reads=[t_rstd[n]], writes=[t_rstd[n]], out=rstd_sb[n][:], in_=rstd_sb[n][:])
        for c in range(DC):
            t, tt = tA()
            OP("dve", "tensor_tensor", reads=z_tiles[c] + [t_mean[n]], writes=[tt], out=t[:], in0=z_aps[c], in1=mean_sb[n][:], op=ALU.subtract)
            OP("dve", "tensor_tensor", reads=[tt, t_rstd[n]], writes=[tt], out=t[:], in0=t[:], in1=rstd_sb[n][:], op=ALU.mult)
            emit_out(c, n, t, tt)

    def ln_out_stream(gname, bname):
        def f(c, n, t, tt):
            OP("act", "activation", reads=[tt, t_const], writes=[t_xres[c][n]], out=xres[:, c, ns(n)], in_=t[:], func=AF.Identity,
               bias=vcol(bname, c), scale=vcol(gname, c))
            OP("pool", "tensor_copy", reads=[t_xres[c][n]], writes=[t_xb[c][n]], out=xb[:, c, ns(n)], in_=xres[:, c, ns(n)])
        return f

    def proj_ln(ukey, rhs_fn, bias_name, gname, bname):
        for j in range(4):
            s = use((ukey, j))
            sv = s.v3(8)
            for mi in range(2):
                c = 2 * j + mi
                for n in range(NSUB):
                    b, tb_ = next_bank()
                    items = []
                    for k in range(DC):
                        r, rt = rhs_fn(k, n)
                        items.append((sv[:, k, mi * 128:(mi + 1) * 128], r, [s.tile] + rt))
                    if bias_name is not None:
                        OP("pool", "tensor_scalar", reads=[t_xres[c][n], t_const], writes=[t_xres[c][n]], out=xres[:, c, ns(n)],
                           in0=xres[:, c, ns(n)], scalar1=ALPHA, scalar2=vcol(bias_name, c), op0=ALU.mult, op1=ALU.add)
                    mm_group(b, tb_, items)
                    if bias_name is not None:
                        OP("dve", "tensor_tensor", reads=[tb_, t_xres[c][n]], writes=[t_xres[c][n]], out=xres[:, c, ns(n)],
                           in0=b, in1=xres[:, c, ns(n)], op=ALU.add)
                    else:
                        OP("dve", "scalar_tensor_tensor", reads=[tb_, t_xres[c][n]], writes=[t_xres[c][n]], out=xres[:, c, ns(n)],
                           in0=xres[:, c, ns(n)], scalar=ALPHA, in1=b, op0=ALU.mult, op1=ALU.add)
                    ln_accum(xres[:, c, ns(n)], [t_xres[c][n]], c, n)
            done(s)
        for n in range(NSUB):
            ln_finish(n, [xres[:, c, ns(n)] for c in range(DC)], [[t_xres[c][n]] for c in range(DC)], ln_out_stream(gname, bname))

    def xb_rhs(k, n):
        return xb[:, k, ns(n)], [t_xb[k][n]]

    def hid(m, n):
        lo = m * 2048 + n * 1024
        return hv_bf(lo, NT), hg(lo, lo + 1024)

    def load_p(l, t0):
        for n in range(NSUB):
            P.dma("sp", ch_p, pst[:], pT[l].rearrange("(k p) t -> p k t", p=128)[:, :, t0 + n * NT:t0 + (n + 1) * NT],
                  writes=[t_pst])
            OP("pool", "tensor_copy", reads=[t_pst], writes=[t_pb16[n]], out=pb16[:, :, ns(n)], in_=pst[:])

    def mlp_ple(l, t0, is_last):
        load_p(l, t0)
        for j in range(16):
            s = use(("up", l, j))
            sv = s.v3(8)
            for mi in range(2):
                m = 2 * j + mi
                for n in range(NSUB):
                    b, tb_ = next_bank()
                    mm_group(b, tb_, [(sv[:, k, mi * 128:(mi + 1) * 128], xb[:, k, ns(n)], [s.tile, t_xb[k][n]]) for k in range(DC)])
                    t, tt = tA()
                    h_ap, h_t = hid(m, n)
                    OP("act", "activation", reads=[tb_], writes=[tt], out=t[:], in_=b, func=AF.Relu)
                    OP("pool", "tensor_tensor", reads=[tt], writes=h_t, out=h_ap, in0=t[:], in1=t[:], op=ALU.mult)
            done(s)
        gname, bname = f"mlp_g{l}", f"mlp_b{l}"
        for m in range(DC):
            s0 = use(("down", l, m, 0))
            s1 = use(("down", l, m, 1))
            for n in range(NSUB):
                b, tb_ = next_bank()
                items = []
                for k in range(FC):
                    s = s0 if k < 16 else s1
                    h_ap, h_t = hid(k, n)
                    items.append((s.v3(16)[:, k % 16, :], h_ap, [s.tile] + h_t))
                mm_group(b, tb_, items)
                OP("dve", "scalar_tensor_tensor", reads=[tb_, t_xres[m][n]], writes=[t_xres[m][n]], out=xres[:, m, ns(n)],
                   in0=xres[:, m, ns(n)], scalar=ALPHA, in1=b, op0=ALU.mult, op1=ALU.add)
                ln_accum(xres[:, m, ns(n)], [t_xres[m][n]], m, n)
            done(s0, s1)
        for n in range(NSUB):
            ln_finish(n, [xres[:, c, ns(n)] for c in range(DC)], [[t_xres[c][n]] for c in range(DC)], ln_out_stream(gname, bname))
        sp_ = use(("proj", l))
        spv = sp_.v3(2)
        for j in range(4):
            sg = use(("gate", l, j))
            sgv = sg.v3(8)
            for mi in range(2):
                c = 2 * j + mi
                for n in range(NSUB):
                    bg, tbg = next_bank()
                    mm_group(bg, tbg, [(sgv[:, k, mi * 128:(mi + 1) * 128], xb[:, k, ns(n)], [sg.tile, t_xb[k][n]]) for k in range(DC)])
                    bp, tbp = next_bank()
                    mm_group(bp, tbp, [(spv[:, k, c * 128:(c + 1) * 128], pb16[:, k, ns(n)], [sp_.tile, t_pb16[n]]) for k in range(2)])
                    t, tt = tA()
                    OP("act", "activation", reads=[tbg], writes=[tt], out=t[:], in_=bg, func=AF.Sigmoid)
                    t2, tt2 = tB()
                    OP("dve", "tensor_tensor", reads=[tbp, tt], writes=[tt2], out=t2[:], in0=bp, in1=t[:], op=ALU.mult)
                    OP("pool", "tensor_tensor", reads=[tt2, t_xres[c][n]], writes=[t_xres[c][n]], out=xres[:, c, ns(n)],
                       in0=xres[:, c, ns(n)], in1=t2[:], op=ALU.add)
            done(sg)
        done(sp_)
        if not is_last:
            for c in range(DC):
                for n in range(NSUB):
                    OP("pool", "tensor_copy", reads=[t_xres[c][n]], writes=[t_xb[c][n]], out=xb[:, c, ns(n)], in_=xres[:, c, ns(n)])

    HB_STRIDE = 3072
    ZC0 = DC * HB_STRIDE

    def hbuf(c, a, b):
        return hv_bf(c * HB_STRIDE, 1536)[:, a:b]

    def t_hbuf(c):
        return hg(c * HB_STRIDE, (c + 1) * HB_STRIDE)

    def zc(c, n):
        lo = ZC0 + c * 4096 + n * 2048
        return hv_f32(lo, NT), hg(lo, lo + 2048)

    def conv_layer(mt):
        for c in range(DC):
            if mt == 0:
                OP("pool", "memset", writes=t_hbuf(c), ap=hbuf(c, 0, 30), constant=0.0)
            else:
                OP("pool", "tensor_copy", reads=[t_halo[c]], writes=t_hbuf(c), out=hbuf(c, 0, 30), in_=halo[:, c, 0:30])
        for j in range(4):
            sa = use(("win_a", j))
            sg = use(("win_g", j))
            sav, sgv = sa.v3(8), sg.v3(8)
            for mi in range(2):
                c = 2 * j + mi
                for n in range(NSUB):
                    ba, tba = next_bank()
                    mm_group(ba, tba, [(sav[:, k, mi * 128:(mi + 1) * 128], xb[:, k, ns(n)], [sa.tile, t_xb[k][n]]) for k in range(DC)])
                    bg, tbg = next_bank()
                    mm_group(bg, tbg, [(sgv[:, k, mi * 128:(mi + 1) * 128], xb[:, k, ns(n)], [sg.tile, t_xb[k][n]]) for k in range(DC)])
                    t, tt = tA()
                    OP("act", "activation", reads=[tbg, t_const], writes=[tt], out=t[:], in_=bg, func=AF.Sigmoid, bias=vcol("b_in_g", c), scale=1.0)
                    OP("dve", "scalar_tensor_tensor", reads=[tba, tt, t_const], writes=t_hbuf(c), out=hbuf(c, 30 + n * NT, 30 + (n + 1) * NT),
                       in0=ba, scalar=vcol("b_in_a", c), in1=t[:], op0=ALU.add, op1=ALU.mult)
            done(sa, sg)
        for c in range(DC):
            di = c % 2
            wv = vecs[:, V_WDW + c * 31:V_WDW + (c + 1) * 31]
            OP("pool", "tensor_tensor", reads=[t_const], writes=[t_diag[di]], out=diag[di][:],
               in0=identf[:].unsqueeze(1).to_broadcast([128, 31, 128]), in1=wv.unsqueeze(2).to_broadcast([128, 31, 128]), op=ALU.mult)
            for n in range(NSUB):
                b, tb_ = next_bank()
                mm_group(b, tb_, [(diag[di][:, tap, :], hbuf(c, n * NT + tap, n * NT + tap + NT), [t_diag[di]] + t_hbuf(c)) for tap in range(31)])
                z_ap, z_t = zc(c, n)
                OP("act", "activation", reads=[tb_, t_const], writes=z_t, out=z_ap, in_=b, func=AF.Identity, bias=vcol("b_dw", c), scale=1.0)
                ln_accum(z_ap, z_t, c, n)
            OP("pool", "tensor_copy", reads=t_hbuf(c), writes=[t_halo[c]], out=halo[:, c, 0:30], in_=hbuf(c, TT, TT + 30))

        def conv_out(c, n, t, tt):
            OP("act", "activation", reads=[tt, t_const], writes=[t_xb[c][n]], out=xb[:, c, ns(n)], in_=t[:], func=AF.Silu,
               bias=vcol("cln_b", c), scale=vcol("cln_g", c))
        for n in range(NSUB):
            ln_finish(n, [zc(c, n)[0] for c in range(DC)], [zc(c, n)[1] for c in range(DC)], conv_out)
        proj_ln("cout", xb_rhs, "b_out", "mix_g0", "mix_b0")

    A_QT = 0
    A_OT = 16384
    A_CS = 32768
    A_SM = 40960
    A_PB = 49152
    A_PT = 53248
    A_OS = 57344

    def qT(c, a, b):
        return hv_bf(A_QT + c * 2048, TT)[:, a:b]

    def t_qT(c, n):
        lo = A_QT + c * 2048 + n * 1024
        return hg(lo, lo + 1024)

    def oT_ap(c, a, b):
        return hv_bf(A_OT + c * 2048, TT)[:, a:b]

    def t_oT(c, n):
        lo = A_OT + c * 2048 + n * 1024
        return hg(lo, lo + 1024)

    def cs_ap(which, n):
        lo = A_CS + which * 4096 + n * 2048
        return hv_f32(lo, NT)

    t_cs = hg(A_CS, A_CS + 8192) if L1 else None

    def attn_layer(mt, t0):
        P.dma("sp", ch_cs, hv_f32(A_CS, 2048).rearrange("p (a t) -> p a t", a=2),
              cs_d.rearrange("a p t -> p a t")[:, :, t0:t0 + TT], writes=t_cs)
        sk, skr, sv_ = use(("k", 0)), use(("kr", 0)), use(("v", 0))
        for kc in range(2):
            for n in range(NSUB):
                bk, tbk = next_bank()
                mm_group(bk, tbk, [(sk.v3(8)[:, k, kc * 128:(kc + 1) * 128], xb[:, k, ns(n)], [sk.tile, t_xb[k][n]]) for k in range(DC)])
                br, tbr = next_bank()
                mm_group(br, tbr, [(skr.v3(8)[:, k, kc * 128:(kc + 1) * 128], xb[:, k, ns(n)], [skr.tile, t_xb[k][n]]) for k in range(DC)])
                t1, tt1 = tA()
                t2, tt2 = tA()
                OP("dve", "tensor_tensor", reads=[tbk] + t_cs, writes=[tt1], out=t1[:], in0=bk, in1=cs_ap(0, n), op=ALU.mult)
                OP("dve", "tensor_tensor", reads=[tbr] + t_cs, writes=[tt2], out=t2[:], in0=br, in1=cs_ap(1, n), op=ALU.mult)
                for hf in range(2):
                    g = kc * 2 + hf
                    rows = slice(hf * 64, (hf + 1) * 64)
                    OP("pool", "tensor_tensor", reads=[tt1, tt2], writes=[t_kTz[g][1 + 4 * n + b4] for b4 in range(4)],
                       out=kTz[g][rows, 128 + n * NT:128 + (n + 1) * NT], in0=t1[rows, :], in1=t2[rows, :], op=ALU.add)
        svv = sv_.v3(8)
        for tb in range(8):
            b, tb_ = next_bank()
            n = tb // 4
            mm_group(b[:, 0:256], tb_, [(xb[:, k, tb * 128:(tb + 1) * 128], svv[:, k, :], [sv_.tile, t_xb[k][n]]) for k in range(DC)])
            OP("act", "copy", reads=[tb_], writes=[t_v[1 + tb]], out=v_sb[:, 1 + tb, :], in_=b[:, 0:256])
        done(sk, skr, sv_)
        for j in range(4):
            sq, sqr = use(("q", j)), use(("qr", j))
            for mi in range(2):
                c = 2 * j + mi
                for n in range(NSUB):
                    bq, tbq = next_bank()
                    mm_group(bq, tbq, [(sq.v3(8)[:, k, mi * 128:(mi + 1) * 128], xb[:, k, ns(n)], [sq.tile, t_xb[k][n]]) for k in range(DC)])
                    br, tbr = next_bank()
                    mm_group(br, tbr, [(sqr.v3(8)[:, k, mi * 128:(mi + 1) * 128], xb[:, k, ns(n)], [sqr.tile, t_xb[k][n]]) for k in range(DC)])
                    t1, tt1 = tA()
                    t2, tt2 = tA()
                    OP("dve", "tensor_tensor", reads=[tbq] + t_cs, writes=[tt1], out=t1[:], in0=bq, in1=cs_ap(0, n), op=ALU.mult)
                    OP("dve", "tensor_tensor", reads=[tbr] + t_cs, writes=[tt2], out=t2[:], in0=br, in1=cs_ap(1, n), op=ALU.mult)
                    OP("pool", "tensor_tensor", reads=[tt1, tt2], writes=t_qT(c, n), out=qT(c, n * NT, (n + 1) * NT), in0=t1[:], in1=t2[:], op=ALU.add)
            done(sq, sqr)
        PT_full = pst_[0][:, 0:512].bitcast(BF16)
        OT_full = pst_[0][:, 512:1024].bitcast(BF16)
        O_ps = pst_[1]
        it = 0
        for qb in range(8):
            n = qb // 4
            ob = qb % 2
            o_sb = hv_bf(A_OS + ob * 2048, 1024)
            t_osb = hg(A_OS + ob * 2048, A_OS + ob * 2048 + 2048)
            for g in range(4):
                r = it % 2
                it += 1
                half = g % 2
                cbase = 4 * (g // 2)
                S_ps = pst_[2 + r][:, :].rearrange("p (j s) -> p j s", j=4)
                tS = [t_bank[4 + 2 * r], t_bank[5 + 2 * r]]
                sm = hv_f32(A_SM + r * 4096, 1024).rearrange("p (j s) -> p j s", j=4)
                t_sm = hg(A_SM + r * 4096, A_SM + r * 4096 + 4096)
                pbv = hv_bf(A_PB + r * 2048, 1024).rearrange("p (j s) -> p j s", j=4)
                t_pb = hg(A_PB + r * 2048, A_PB + r * 2048 + 2048)
                ptv = hv_bf(A_PT + r * 2048, 1024)
                t_pt = hg(A_PT + r * 2048, A_PT + r * 2048 + 2048)
                sx = stt[r]
                tsx = t_stt[r]
                for j in range(4):
                    c = cbase + j
                    OP("pe", "matmul", reads=t_qT(c, n) + [t_kTz[g][qb], t_kTz[g][qb + 1]], writes=tS, out=S_ps[:, j, :],
                       lhsT=qT(c, qb * 128, (qb + 1) * 128), rhs=kTz[g][:, qb * 128:qb * 128 + 256], start=True, stop=True)
                mi_ = 1 if (mt == 0 and qb == 0) else 0
                OP("dve", "tensor_tensor", reads=tS + [t_const], writes=t_sm, out=sm, in0=S_ps,
                   in1=msk[:, mi_, :].unsqueeze(1).to_broadcast([128, 4, 256]), op=ALU.add)
                OP("dve", "tensor_reduce", reads=t_sm, writes=[tsx], out=sx[:, 0:4], in_=sm, axis=AX.X, op=ALU.max)
                sinkg = vecs[:, V_SINK + 4 * g:V_SINK + 4 * g + 4]
                OP("dve", "scalar_tensor_tensor", reads=[tsx, t_const], writes=[tsx], out=sx[:, 4:8], in0=sx[:, 0:4], scalar=0.125, in1=sinkg,
                   op0=ALU.mult, op1=ALU.max)
                OP("dve", "tensor_scalar", reads=[tsx], writes=[tsx], out=sx[:, 8:12], in0=sx[:, 4:8], scalar1=-1.0, scalar2=None, op0=ALU.mult)
                OP("dve", "tensor_tensor", reads=[tsx, t_const], writes=[tsx], out=sx[:, 16:20], in0=sinkg, in1=sx[:, 4:8], op=ALU.subtract)
                OP("dve", "memset", reads=[tsx], writes=[tsx], ap=sx[:, 12:16], constant=0.0)
                for j in range(4):
                    OP("act", "activation", reads=t_sm + [tsx], writes=t_pb + [tsx], out=pbv[:, j, :], in_=sm[:, j, :], func=AF.Exp,
                       bias=sx[:, 8 + j:9 + j], scale=0.125, accum_out=sx[:, 12 + j:13 + j])
                OP("act", "activation", reads=[tsx], writes=[tsx], out=sx[:, 20:24], in_=sx[:, 16:20], func=AF.Exp)
                OP("dve", "tensor_tensor", reads=[tsx], writes=[tsx], out=sx[:, 24:28], in0=sx[:, 12:16], in1=sx[:, 20:24], op=ALU.add)
                OP("dve", "reciprocal", reads=[tsx], writes=[tsx], out=sx[:, 28:32], in_=sx[:, 24:28])
                for j in range(4):
                    for hf in range(2):
                        o_ = (j * 2 + hf) * 128
                        OP("pe", "transpose", reads=t_pb + [t_const], writes=[t_bank[0]], out=PT_full[:, o_:o_ + 128],
                           in_=pbv[:, j, hf * 128:(hf + 1) * 128], identity=identb[:])
                OP("act", "copy", reads=[t_bank[0]], writes=t_pt, out=ptv, in_=PT_full)
                for j in range(4):
                    pos = (cbase + j) * 2 + half
                    for hf in range(2):
                        o_ = (j * 2 + hf) * 128
                        OP("pe", "matmul", reads=t_pt + [t_v[qb + hf]], writes=[t_bank[2], t_bank[3]], out=O_ps[:, pos * 64:(pos + 1) * 64],
                           lhsT=ptv[:, o_:o_ + 128], rhs=v_sb[:, qb + hf, g * 64:(g + 1) * 64], start=(hf == 0), stop=(hf == 1))
                ov = O_ps[:, :].rearrange("p (c two e) -> p c two e", two=2, e=64)[:, cbase:cbase + 4, half, :]
                osv = o_sb.rearrange("p (c two e) -> p c two e", two=2, e=64)[:, cbase:cbase + 4, half, :]
                OP("dve", "tensor_tensor", reads=[t_bank[2], t_bank[3], tsx], writes=t_osb, out=osv, in0=ov,
                   in1=sx[:, 28:32].unsqueeze(2).to_broadcast([128, 4, 64]), op=ALU.mult)
            for c in range(DC):
                OP("pe", "transpose", reads=t_osb + [t_const], writes=[t_bank[1]], out=OT_full[:, c * 128:(c + 1) * 128],
                   in_=o_sb[:, c * 128:(c + 1) * 128], identity=identb[:])
            q4 = qb % 4
            dst = hv_bf(A_OT, DC * TT).rearrange("p (c t) -> p c t", c=DC)[:, :, qb * 128:(qb + 1) * 128]
            OP("act", "copy", reads=[t_bank[1]], writes=[g_ for c in range(DC) for g_ in t_oT(c, n)], out=dst,
               in_=OT_full.rearrange("p (c t) -> p c t", c=DC))
        for g in range(4):
            OP("pool", "tensor_copy", reads=[t_kTz[g][8]], writes=[t_kTz[g][0]], out=kTz[g][:, 0:128], in_=kTz[g][:, TT:TT + 128])
        OP("pool", "tensor_copy", reads=[t_v[8]], writes=[t_v[0]], out=v_sb[:, 0, :], in_=v_sb[:, 8, :])

        def o_rhs(k, n):
            return oT_ap(k, n * NT, (n + 1) * NT), t_oT(k, n)
        proj_ln("o", o_rhs, None, "mix_g1", "mix_b1")

    if L1:
        for g in range(4):
            OP("pool", "memset", writes=t_kTz[g], ap=kTz[g][:], constant=0.0)
        OP("pool", "memset", writes=t_v, ap=v_sb[:], constant=0.0)
    all_xres = [t_xres[c][n] for c in range(DC) for n in range(NSUB)]
    for mt in range(nmt):
        t0 = mt * TT
        P.dma("sp", ch_x, xres[:], xT.rearrange("(c p) t -> p c t", p=128)[:, :, t0:t0 + TT], writes=all_xres)
        for c in range(DC):
            for n in range(NSUB):
                eng = "pool" if (c + n) % 2 == 0 else "dve"
                OP(eng, "tensor_copy", reads=[t_xres[c][n]], writes=[t_xb[c][n]], out=xb[:, c, ns(n)], in_=xres[:, c, ns(n)])
        for l in layers:
            if l == 0:
                conv_layer(mt)
            else:
                attn_layer(mt, t0)
            mlp_ple(l, t0, l == last_layer)
        for n in range(NSUB):
            P.dma("sp", ch_out[n], outT.rearrange("(c p) t -> p c t", p=128)[:, :, t0 + n * NT:t0 + (n + 1) * NT],
                  xres[:, :, ns(n)], reads=[t_xres[c][n] for c in range(DC)], writes=[Tile("o")])
    assert st["pos"] == len(order), (st["pos"], len(order))
    P.finalize(final_waits=ch_out)
    P.close()
    return nc


def _fm(v):
    return np.ascontiguousarray(np.asarray(v, np.float32).reshape(8, 128).T)


def _rope_tables(tlen):
    pos = np.arange(tlen, dtype=np.float32)
    inv_freq = (np.float32(ROPE_THETA) ** (-np.arange(0, 16, 2, dtype=np.float32) / np.float32(16))).astype(np.float32)
    ang = (pos[:, None] * inv_freq[None, :]).astype(np.float32)
    cos, sin = np.cos(ang).astype(np.float32), np.sin(ang).astype(np.float32)
    ct = np.ones((128, tlen), np.float32)
    stb = np.zeros((128, tlen), np.float32)
    for p in range(128):
        d = p % 64
        if d < 8:
            ct[p] = cos[:, d]
            stb[p] = -sin[:, d]
        elif d < 16:
            ct[p] = cos[:, d - 8]
            stb[p] = sin[:, d - 8]
    return np.stack([ct, stb])


def _rot_cols(w, nheads):
    idx = np.arange(nheads * 64).reshape(nheads, 64).copy()
    src = idx.copy()
    src[:, 0:8] = idx[:, 8:16]
    src[:, 8:16] = idx[:, 0:8]
    return w[:, src.reshape(-1)]


def _prep(inp, layers=(0, 1)):
    f = lambda a: np.ascontiguousarray(np.asarray(a, dtype=np.float32))
    shared = {}
    if 0 in layers:
        shared["w_in"] = f(inp["conv_w_in"][0])
        shared["w_cout"] = f(inp["conv_w_out"][0])
    if 1 in layers:
        wk, wv, wq, wo = f(inp["kv_w_k"]), f(inp["kv_w_v"]), f(inp["attn_w_q"][0]), f(inp["attn_w_o"][0])
        shared["w_kk"] = np.ascontiguousarray(np.stack([wk, _rot_cols(wk, 4), wv]))
        colperm = np.concatenate([np.arange(h * 64, (h + 1) * 64) for h in PERM_HEADS])
        shared["w_qq"] = np.ascontiguousarray(np.stack([wq[:, colperm], _rot_cols(wq, 16)[:, colperm]]))
        shared["w_o"] = np.ascontiguousarray(wo[colperm, :])
        a = np.arange(128)[:, None]
        s = np.arange(256)[None, :]
        ok = (s > a) & (s <= a + 128)
        mg = np.where(ok, 0.0, NEG).astype(np.float32)
        mf = np.where(ok & (s >= 128), 0.0, NEG).astype(np.float32)
        shared["msk"] = np.ascontiguousarray(np.stack([mg, mf]))
    shared["w_up"] = f(inp["mlp_w_up"])
    shared["w_down"] = f(inp["mlp_w_down"])
    shared["w_proj"] = f(inp["ple_w_proj"])
    shared["w_gate"] = f(inp["ple_w_gate"])
    vec = np.zeros((128, NV), np.float32)

    def put(name, v):
        vec[:, VOFF[name]:VOFF[name] + 8] = _fm(v)
    b_in = f(inp["conv_b_in"][0])
    put("b_in_a", b_in[:1024]); put("b_in_g", b_in[1024:])
    put("b_dw", inp["conv_b_dw"][0]); put("cln_g", inp["conv_ln_g"][0]); put("cln_b", inp["conv_ln_b"][0])
    put("b_out", inp["conv_b_out"][0])
    for l in range(2):
        put(f"mix_g{l}", inp["mix_ln_g"][l]); put(f"mix_b{l}", inp["mix_ln_b"][l])
        put(f"mlp_g{l}", inp["mlp_ln_g"][l]); put(f"mlp_b{l}", inp["mlp_ln_b"][l])
    wdw = f(inp["conv_w_dw"][0])
    vec[:, V_WDW:V_WDW + 248] = wdw.T.reshape(8, 128, 31).transpose(1, 0, 2).reshape(128, 248)
    vec[:, V_SINK:V_SINK + 16] = np.broadcast_to(f(inp["attn_sinks"][0])[None, :], (128, 16))
    vec[:, V_EPS] = EPS
    shared["vecs"] = vec
    shared["ident"] = np.eye(128, dtype=np.float32)
    return shared


def _run(inp_x, inp_p, shared, layers, nmt=4, ncores=8):
    tlen = nmt * TT
    nc = build(layers=layers, nmt=nmt, tlen=tlen)
    if 1 in layers:
        shared = dict(shared)
        shared["cs"] = _rope_tables(tlen)
    in_maps = []
    for b in range(ncores):
        m = dict(shared)
        m["xT"] = np.ascontiguousarray(inp_x[b][:tlen].T)
        m["pT"] = np.ascontiguousarray(np.transpose(inp_p[:, b, :tlen, :], (0, 2, 1)))
        in_maps.append(m)
    res = run_bass_kernel_spmd(nc, in_maps, core_ids=list(range(ncores)))
    return [np.ascontiguousarray(r["outT"].T) for r in res.results]


def kernel(**inputs):
    x = np.asarray(inputs["x"], np.float32)
    p = np.asarray(inputs["p"], np.float32)
    shared = _prep(inputs, (0, 1))
    outs = _run(x, p, shared, (0, 1), nmt=4, ncores=8)
    return np.stack(outs).astype(np.float32)
```

```python
import numpy as np
import concourse.bass as bass
import concourse.mybir as mybir
from concourse.bass_utils import run_bass_kernel_spmd

F32 = mybir.dt.float32
BF16 = mybir.dt.bfloat16
AF = mybir.ActivationFunctionType
ALU = mybir.AluOpType
AX = mybir.AxisListType

D = 1024
T = 4096
TT = 1024
NT = 512
NSUB = 2
DC = 8
FC = 32
NHEAD = 16
ALPHA = float(4.0 ** 0.25)
EPS = 1e-5
NEG = -30000.0
ROPE_THETA = 500000.0
PERM_HEADS = [0, 4, 1, 5, 2, 6, 3, 7, 8, 12, 9, 13, 10, 14, 11, 15]

ENGS = ("pe", "act", "dve", "pool", "sp")
STRICT_SAME_ENGINE = True


class Tile:
    __slots__ = ("name", "lw", "rd")

    def __init__(self, name):
        self.name = name
        self.lw = None
        self.rd = {}


class Op:
    __slots__ = ("eng", "fn", "deps", "marked", "seq", "is_dma", "chan", "dcount")

    def __init__(self, eng, fn, is_dma=False):
        self.eng = eng
        self.fn = fn
        self.deps = []
        self.marked = False
        self.seq = None
        self.is_dma = is_dma
        self.chan = None
        self.dcount = None


class Chan:
    def __init__(self, sem, name):
        self.sem = sem
        self.count = 0
        self.name = name


class Prog:
    def __init__(self, nc):
        self.nc = nc
        self.ops = []
        self.q = {e: [] for e in ENGS}
        self.sems = {}
        self._ctx = []

    def enter(self, cm):
        v = cm.__enter__()
        self._ctx.append(cm)
        return v

    def sbuf(self, name, shape, dtype):
        return self.enter(self.nc.sbuf_tensor("sb_" + name, list(shape), dtype))

    def psum(self, name, shape, dtype):
        return self.enter(self.nc.psum_tensor("pp_" + name, list(shape), dtype))

    def chan(self, name):
        s = self.enter(self.nc.semaphore("c_" + name))
        return Chan(s, name)

    def close(self):
        for cm in reversed(self._ctx):
            cm.__exit__(None, None, None)
        self._ctx = []

    def _track(self, op, reads, writes):
        deps = op.deps
        for t in reads:
            if t.lw is not None:
                deps.append(("raw", t.lw))
        for t in writes:
            if t.lw is not None:
                deps.append(("waw", t.lw))
            for r in t.rd.values():
                if isinstance(r, list):
                    for x in r:
                        deps.append(("war", x))
                else:
                    deps.append(("war", r))
        for t in reads:
            if op.is_dma:
                t.rd.setdefault("dma", []).append(op)
            else:
                t.rd[op.eng] = op
        for t in writes:
            t.lw = op
            t.rd = {}

    def op(self, eng, fn, reads=(), writes=()):
        o = Op(eng, fn)
        self._track(o, reads, writes)
        self.ops.append(o)
        self.q[eng].append(o)
        return o

    def dma(self, eng, chan, out, in_, reads=(), writes=()):
        def fn(e, out=out, in_=in_):
            return e.dma_start(out=out, in_=in_)
        o = Op(eng, fn, is_dma=True)
        o.chan = chan
        chan.count += 16
        o.dcount = chan.count
        self._track(o, reads, writes)
        self.ops.append(o)
        self.q[eng].append(o)
        return o

    def finalize(self, final_waits=()):
        nc = self.nc
        for e in ENGS:
            self.sems[e] = self.enter(nc.semaphore("p_" + e))
        for o in self.ops:
            real = []
            for kind, d in o.deps:
                if d.is_dma:
                    real.append(d)
                    continue
                if d.eng == o.eng and not o.is_dma:
                    if d.eng == "pe":
                        continue
                    if kind != "raw" and not STRICT_SAME_ENGINE:
                        continue
                d.marked = True
                real.append(d)
            o.deps = real
        cnt = {e: 0 for e in ENGS}
        for e in ENGS:
            for o in self.q[e]:
                if o.marked and not o.is_dma:
                    cnt[e] += 1
                    o.seq = cnt[e]
        known = {e: {} for e in ENGS}
        emit = {e: [] for e in ENGS}
        nw = 0
        for o in self.ops:
            need = {}
            for d in o.deps:
                if d.is_dma:
                    key = ("c", id(d.chan))
                    sem, tgt = d.chan.sem, d.dcount
                else:
                    key = ("e", d.eng)
                    sem, tgt = self.sems[d.eng], d.seq
                if known[o.eng].get(key, 0) >= tgt:
                    continue
                if key not in need or need[key][1] < tgt:
                    need[key] = (sem, tgt)
            for key, (sem, tgt) in need.items():
                known[o.eng][key] = tgt
                emit[o.eng].append(("wait", sem, tgt))
                nw += 1
            emit[o.eng].append(("op", o))
        self.nwaits = nw
        self.counts = cnt
        sems = self.sems

        def run(engobj, ekey):
            for it in emit[ekey]:
                if it[0] == "wait":
                    engobj.wait_ge(it[1], it[2])
                else:
                    o = it[1]
                    ins = o.fn(engobj)
                    if o.is_dma:
                        ins.then_inc(o.chan.sem, 16)
                    elif o.marked:
                        ins.then_inc(sems[ekey], 1)
            if ekey == "sp":
                for ch in final_waits:
                    engobj.wait_ge(ch.sem, ch.count)

        with nc.Block() as block:
            @block.tensor
            def _(e):
                run(e, "pe")

            @block.scalar
            def _(e):
                run(e, "act")

            @block.vector
            def _(e):
                run(e, "dve")

            @block.gpsimd
            def _(e):
                run(e, "pool")

            @block.sync
            def _(e):
                run(e, "sp")


VEC_NAMES = ["b_in_a", "b_in_g", "b_dw", "cln_g", "cln_b", "b_out",
             "mix_g0", "mix_b0", "mlp_g0", "mlp_b0", "mix_g1", "mix_b1", "mlp_g1", "mlp_b1"]
VOFF = {n: i * 8 for i, n in enumerate(VEC_NAMES)}
V_WDW = len(VEC_NAMES) * 8
V_SINK = V_WDW + 8 * 31
V_EPS = V_SINK + 16
NV = V_EPS + 1


def _unit_table(layers):
    units = []
    idx = {}

    def add(key, kind, arg):
        idx[key] = len(units)
        units.append((key, kind, arg))

    if 0 in layers:
        for j in range(4):
            add(("win_a", j), "A", ("w_in", None, j * 256))
            add(("win_g", j), "A", ("w_in", None, 1024 + j * 256))
        for j in range(4):
            add(("cout", j), "A", ("w_cout", None, j * 256))
    if 1 in layers:
        for i, nm in enumerate(("k", "kr", "v")):
            add((nm, 0), "A", ("w_kk", i, 0))
        for j in range(4):
            add(("q", j), "A", ("w_qq", 0, j * 256))
            add(("qr", j), "A", ("w_qq", 1, j * 256))
        for j in range(4):
            add(("o", j), "A", ("w_o", None, j * 256))
    for l in layers:
        for j in range(16):
            add(("up", l, j), "A", ("w_up", l, j * 256))
        for m in range(8):
            for h in range(2):
                add(("down", l, m, h), "DN", ("w_down", l, m, h))
        for j in range(4):
            add(("gate", l, j), "A", ("w_gate", l, j * 256))
        add(("proj", l), "PJ", ("w_proj", l))
    return units, idx


def _layer_order(idx, l):
    o = []
    if l == 0:
        for j in range(4):
            o += [idx[("win_a", j)], idx[("win_g", j)]]
        o += [idx[("cout", j)] for j in range(4)]
    else:
        o += [idx[("k", 0)], idx[("kr", 0)], idx[("v", 0)]]
        for j in range(4):
            o += [idx[("q", j)], idx[("qr", j)]]
        o += [idx[("o", j)] for j in range(4)]
    o += [idx[("up", l, j)] for j in range(16)]
    for m in range(8):
        o += [idx[("down", l, m, 0)], idx[("down", l, m, 1)]]
    o.append(idx[("proj", l)])
    o += [idx[("gate", l, j)] for j in range(4)]
    return o


class Slot:
    __slots__ = ("ap", "tile", "idx")

    def __init__(self, ap, tile, idx):
        self.ap = ap
        self.tile = tile
        self.idx = idx

    def v3(self, kc):
        return self.ap[:, :].rearrange("p (k m) -> p k m", k=kc)


def build(layers=(0, 1), nmt=4, tlen=T):
    nc = bass.Bass("TRN2", target_bir_lowering=False)
    P = Prog(nc)
    L0 = 0 in layers
    L1 = 1 in layers
    last_layer = layers[-1]

    def din(name, shape):
        return nc.dram_tensor(name, list(shape), F32, kind="ExternalInput").ap()

    xT = din("xT", [D, tlen])
    pT = din("pT", [2, 256, tlen])
    W = {}
    if L0:
        W["w_in"] = din("w_in", [D, 2048])
        W["w_cout"] = din("w_cout", [D, D])
    if L1:
        W["w_kk"] = din("w_kk", [3, D, 256])
        W["w_qq"] = din("w_qq", [2, D, D])
        W["w_o"] = din("w_o", [D, D])
        cs_d = din("cs", [2, 128, tlen])
        msk_d = din("msk", [2, 128, 256])
    W["w_up"] = din("w_up", [2, D, 4096])
    W["w_down"] = din("w_down", [2, 4096, D])
    W["w_proj"] = din("w_proj", [2, 256, D])
    W["w_gate"] = din("w_gate", [2, D, D])
    vecs_d = din("vecs", [128, NV])
    ident_d = din("ident", [128, 128])
    outT = nc.dram_tensor("outT", [D, tlen], F32, kind="ExternalOutput").ap()

    units, uidx = _unit_table(layers)
    NU = len(units)
    wsc = nc.dram_tensor("wsc", [NU, 128, 2048], BF16).ap()
    t_wsc = [Tile(f"wsc{u}") for u in range(NU)]

    def unit_src(u):
        key, kind, arg = units[u]
        if kind == "A":
            nm, li, c0 = arg
            w = W[nm] if li is None else W[nm][li]
            return w.rearrange("(k p) n -> p k n", p=128)[:, :, c0:c0 + 256], 8
        if kind == "DN":
            nm, li, m, h = arg
            w = W[nm][li]
            return w.rearrange("(k p) n -> p k n", p=128)[:, h * 16:(h + 1) * 16, m * 128:(m + 1) * 128], 16
        nm, li = arg
        w = W[nm][li]
        return w.rearrange("(k p) n -> p k n", p=128)[:, :, :], 2

    xres = P.sbuf("xres", [128, DC, TT], F32)
    xb = P.sbuf("xb", [128, DC, TT], BF16)
    HB = 66560
    Hs = P.sbuf("Hs", [128, HB // 2], BF16)
    NG = HB // 1024
    tH = [Tile(f"H{g}") for g in range(NG)]

    def hg(lo, hi):
        return tH[lo // 1024:(hi + 1023) // 1024]

    def hv_bf(lo, n):
        return Hs[:, lo // 2:lo // 2 + n]

    def hv_f32(lo, n):
        return Hs[:, lo // 2:lo // 2 + 2 * n].bitcast(F32)

    vecs = P.sbuf("vecs", [128, NV], F32)
    identf = P.sbuf("identf", [128, 128], F32)
    identb = P.sbuf("identb", [128, 128], BF16)
    onesD = P.sbuf("onesD", [128, 128], BF16)
    t_const = Tile("const")
    tmpA = [P.sbuf(f"tmpA{i}", [128, NT], F32) for i in range(4)]
    t_tmpA = [Tile(f"tmpA{i}") for i in range(4)]
    tmpB = [P.sbuf(f"tmpB{i}", [128, NT], F32) for i in range(2)]
    t_tmpB = [Tile(f"tmpB{i}") for i in range(2)]
    zbb = [P.sbuf(f"zb{i}", [128, NT], BF16) for i in range(2)]
    zsb = [P.sbuf(f"zs{i}", [128, NT], BF16) for i in range(2)]
    t_zbb = [Tile(f"zb{i}") for i in range(2)]
    t_zsb = [Tile(f"zs{i}") for i in range(2)]
    mean_sb = [P.sbuf(f"mean{i}", [128, NT], F32) for i in range(2)]
    rstd_sb = [P.sbuf(f"rstd{i}", [128, NT], F32) for i in range(2)]
    t_mean = [Tile(f"mean{i}") for i in range(2)]
    t_rstd = [Tile(f"rstd{i}") for i in range(2)]
    pst = P.sbuf("pst", [128, 2, NT], F32)
    t_pst = Tile("pst")
    pb16 = P.sbuf("pb16", [128, 2, TT], BF16)
    t_pb16 = [Tile(f"pb16_{n}") for n in range(NSUB)]
    if L0:
        diag = [P.sbuf(f"diag{i}", [128, 31, 128], BF16) for i in range(2)]
        t_diag = [Tile(f"diag{i}") for i in range(2)]
        halo = P.sbuf("halo", [128, DC, 32], BF16)
        t_halo = [Tile(f"halo{c}") for c in range(DC)]
    if L1:
        kTz = [P.sbuf(f"kTz{g}", [128, 128 + TT], BF16) for g in range(4)]
        t_kTz = [[Tile(f"kTz{g}_{b}") for b in range(9)] for g in range(4)]
        v_sb = P.sbuf("v_sb", [128, 9, 256], BF16)
        t_v = [Tile(f"v{b}") for b in range(9)]
        msk = P.sbuf("msk", [128, 2, 256], F32)
        stt = [P.sbuf(f"stt{i}", [128, 32], F32) for i in range(2)]
        t_stt = [Tile(f"stt{i}") for i in range(2)]

    t_xres = [[Tile(f"xres{c}_{n}") for n in range(NSUB)] for c in range(DC)]
    t_xb = [[Tile(f"xb{c}_{n}") for n in range(NSUB)] for c in range(DC)]

    pst_ = [P.psum(f"ps{i}", [128, 1024], F32) for i in range(4)]
    t_bank = [Tile(f"bank{i}") for i in range(8)]

    def bank_ap(i):
        return pst_[i // 2][:, (i % 2) * 512:(i % 2) * 512 + 512]

    gbi = [0]

    def next_bank():
        i = gbi[0] % 4
        gbi[0] += 1
        return bank_ap(i), t_bank[i]

    def OP(eng, meth, reads=(), writes=(), **kw):
        return P.op(eng, lambda e: getattr(e, meth)(**kw), reads=reads, writes=writes)

    def vcol(name, c):
        o = VOFF[name] + c
        return vecs[:, o:o + 1]

    ch_x = P.chan("x")
    ch_out = [P.chan(f"out{n}") for n in range(NSUB)]
    ch_c = P.chan("consts")
    ch_p = P.chan("p")
    ch_cs = P.chan("cs")

    P.dma("sp", ch_c, vecs[:], vecs_d[:, :], writes=[t_const])
    P.dma("sp", ch_c, identf[:], ident_d[:, :], writes=[t_const])
    if L1:
        P.dma("sp", ch_c, msk[:], msk_d.rearrange("a p s -> p a s"), writes=[t_const])
    OP("dve", "tensor_copy", reads=[t_const], writes=[t_const], out=identb[:], in_=identf[:])
    OP("dve", "memset", writes=[t_const], ap=onesD[:], constant=1.0 / D)

    bars = {e: Tile("bar_" + e) for e in ENGS}
    barl = list(bars.values())
    NSTG = 4
    stg_f = [hv_f32(i * 8192, 2048) for i in range(NSTG)]
    stg_b = [hv_bf(32768 + i * 4096, 2048) for i in range(NSTG)]
    t_sf = [Tile(f"sf{i}") for i in range(NSTG)]
    t_sb = [Tile(f"sb{i}") for i in range(NSTG)]
    ch_sf = [P.chan(f"sf{i}") for i in range(NSTG)]
    ch_sb = [P.chan(f"sb{i}") for i in range(NSTG)]
    order1 = []
    for l in layers:
        order1 += _layer_order(uidx, l)
    assert sorted(order1) == list(range(NU))
    cast_eng = ("dve", "act", "pool")
    for i, u in enumerate(order1):
        s = i % NSTG
        src, kc = unit_src(u)
        P.dma("sp", ch_sf[s], stg_f[s].rearrange("p (k m) -> p k m", k=kc), src, reads=barl, writes=[t_sf[s]])
        ce = cast_eng[i % 3]
        if ce == "act":
            OP("act", "copy", reads=[t_sf[s]] + barl, writes=[t_sb[s]], out=stg_b[s], in_=stg_f[s])
        else:
            OP(ce, "tensor_copy", reads=[t_sf[s]] + barl, writes=[t_sb[s]], out=stg_b[s], in_=stg_f[s])
        P.dma("act", ch_sb[s], wsc[u, :, :], stg_b[s], reads=[t_sb[s]] + barl, writes=[t_wsc[u]])
    for e in ENGS:
        P.op(e, lambda en: en.nop(), writes=[bars[e]])

    NSLOT = 7
    ws_slots = [P.sbuf(f"ws{i}", [128, 2048], BF16) for i in range(NSLOT)]
    ws_tiles = [Tile(f"ws{i}") for i in range(NSLOT)]
    ws_ch = [P.chan(f"ws{i}") for i in range(NSLOT)]
    order = order1 * nmt
    st = {"issued": 0, "pos": 0, "rel": [True] * NSLOT}

    def pump():
        while st["issued"] < len(order):
            s = st["issued"] % NSLOT
            if not st["rel"][s]:
                break
            u = order[st["issued"]]
            P.dma("sp", ws_ch[s], ws_slots[s][:], wsc[u, :, :], reads=[t_wsc[u]], writes=[ws_tiles[s]])
            st["rel"][s] = False
            st["issued"] += 1

    def use(key):
        u = uidx[key]
        assert order[st["pos"]] == u, (key, st["pos"])
        pump()
        assert st["issued"] > st["pos"]
        s = st["pos"] % NSLOT
        st["pos"] += 1
        return Slot(ws_slots[s], ws_tiles[s], s)

    def done(*slots):
        for sl in slots:
            st["rel"][sl.idx] = True
        pump()

    def ns(n):
        return slice(n * NT, (n + 1) * NT)

    tAi = [0]
    tBi = [0]
    zi = [0]

    def tA():
        i = tAi[0] % 4
        tAi[0] += 1
        return tmpA[i], t_tmpA[i]

    def tB():
        i = tBi[0] % 2
        tBi[0] += 1
        return tmpB[i], t_tmpB[i]

    def mm_group(out_ap, out_tile, items):
        nI = len(items)
        for i, (l, r, rd) in enumerate(items):
            OP("pe", "matmul", reads=rd, writes=[out_tile], out=out_ap, lhsT=l, rhs=r, start=(i == 0), stop=(i == nI - 1))

    def stat_banks(n):
        return (bank_ap(4 + 2 * n), t_bank[4 + 2 * n]), (bank_ap(5 + 2 * n), t_bank[5 + 2 * n])

    def ln_accum(z_ap, z_tiles, c, n):
        (mb, tmb), (qb_, tqb) = stat_banks(n)
        i = zi[0] % 2
        zi[0] += 1
        OP("act", "copy", reads=z_tiles, writes=[t_zbb[i]], out=zbb[i][:], in_=z_ap)
        OP("act", "activation", reads=z_tiles, writes=[t_zsb[i]], out=zsb[i][:], in_=z_ap, func=AF.Square)
        OP("pe", "matmul", reads=[t_zbb[i], t_const], writes=[tmb], out=mb, lhsT=onesD[:], rhs=zbb[i][:], start=(c == 0), stop=(c == DC - 1))
        OP("pe", "matmul", reads=[t_zsb[i], t_const], writes=[tqb], out=qb_, lhsT=onesD[:], rhs=zsb[i][:], start=(c == 0), stop=(c == DC - 1))

    LN_POOL_CHUNKS = (2, 5, 7)

    def ln_finish(n, z_aps, z_tiles, emit_out):
        (mb, tmb), (qb_, tqb) = stat_banks(n)
        OP("act", "copy", reads=[tmb], writes=[t_mean[n]], out=mean_sb[n][:], in_=mb)
        OP("act", "activation", reads=[tmb], writes=[t_rstd[n]], out=rstd_sb[n][:], in_=mb, func=AF.Square)
        OP("dve", "tensor_tensor", reads=[tqb, t_rstd[n]], writes=[t_rstd[n]], out=rstd_sb[n][:], in0=qb_, in1=rstd_sb[n][:], op=ALU.subtract)
        OP("act", "activation", reads=[t_rstd[n], t_const], writes=[t_rstd[n]], out=rstd_sb[n][:], in_=rstd_sb[n][:], func=AF.Sqrt,
           bias=vecs[:, V_EPS:V_EPS + 1], scale=1.0)
        OP("dve", "reciprocal", reads=[t_rstd[n]], writes=[t_rstd[n]], out=rstd_sb[n][:], in_=rstd_sb[n][:])
        for c in range(DC):
            t, tt = tA()
            eng = "pool" if c in LN_POOL_CHUNKS else "dve"
            OP(eng, "tensor_tensor", reads=z_tiles[c] + [t_mean[n]], writes=[tt], out=t[:], in0=z_aps[c], in1=mean_sb[n][:], op=ALU.subtract)
            OP(eng, "tensor_tensor", reads=[tt, t_rstd[n]], writes=[tt], out=t[:], in0=t[:], in1=rstd_sb[n][:], op=ALU.mult)
            emit_out(c, n, t, tt)

    def ln_out_stream(gname, bname):
        def f(c, n, t, tt):
            OP("act", "activation", reads=[tt, t_const], writes=[t_xb[c][n]], out=xb[:, c, ns(n)], in_=t[:], func=AF.Identity,
               bias=vcol(bname, c), scale=vcol(gname, c))
            OP("act", "activation", reads=[tt, t_const], writes=[t_xres[c][n]], out=xres[:, c, ns(n)], in_=t[:], func=AF.Identity,
               bias=vcol(bname, c), scale=vcol(gname, c))
        return f

    def proj_ln(ukey, rhs_fn, bias_name, gname, bname):
        for j in range(4):
            s = use((ukey, j))
            sv = s.v3(8)
            for mi in range(2):
                c = 2 * j + mi
                for n in range(NSUB):
                    b, tb_ = next_bank()
                    items = []
                    for k in range(DC):
                        r, rt = rhs_fn(k, n)
                        items.append((sv[:, k, mi * 128:(mi + 1) * 128], r, [s.tile] + rt))
                    if bias_name is not None:
                        OP("pool", "tensor_scalar", reads=[t_xres[c][n], t_const], writes=[t_xres[c][n]], out=xres[:, c, ns(n)],
                           in0=xres[:, c, ns(n)], scalar1=ALPHA, scalar2=vcol(bias_name, c), op0=ALU.mult, op1=ALU.add)
                    mm_group(b, tb_, items)
                    if bias_name is not None:
                        OP("dve", "tensor_tensor", reads=[tb_, t_xres[c][n]], writes=[t_xres[c][n]], out=xres[:, c, ns(n)],
                           in0=b, in1=xres[:, c, ns(n)], op=ALU.add)
                    else:
                        OP("dve", "scalar_tensor_tensor", reads=[tb_, t_xres[c][n]], writes=[t_xres[c][n]], out=xres[:, c, ns(n)],
                           in0=xres[:, c, ns(n)], scalar=ALPHA, in1=b, op0=ALU.mult, op1=ALU.add)
                    ln_accum(xres[:, c, ns(n)], [t_xres[c][n]], c, n)
            done(s)
        for n in range(NSUB):
            ln_finish(n, [xres[:, c, ns(n)] for c in range(DC)], [[t_xres[c][n]] for c in range(DC)], ln_out_stream(gname, bname))

    def xb_rhs(k, n):
        return xb[:, k, ns(n)], [t_xb[k][n]]

    def hid(m, n):
        lo = m * 2048 + n * 1024
        return hv_bf(lo, NT), hg(lo, lo + 1024)

    def load_p(l, t0):
        for n in range(NSUB):
            P.dma("sp", ch_p, pst[:], pT[l].rearrange("(k p) t -> p k t", p=128)[:, :, t0 + n * NT:t0 + (n + 1) * NT],
                  writes=[t_pst])
            OP("pool", "tensor_copy", reads=[t_pst], writes=[t_pb16[n]], out=pb16[:, :, ns(n)], in_=pst[:])

    def mlp_ple(l, t0, is_last):
        load_p(l, t0)
        for j in range(16):
            s = use(("up", l, j))
            sv = s.v3(8)
            for mi in range(2):
                m = 2 * j + mi
                for n in range(NSUB):
                    b, tb_ = next_bank()
                    mm_group(b, tb_, [(sv[:, k, mi * 128:(mi + 1) * 128], xb[:, k, ns(n)], [s.tile, t_xb[k][n]]) for k in range(DC)])
                    t, tt = tA()
                    h_ap, h_t = hid(m, n)
                    OP("act", "activation", reads=[tb_], writes=[tt], out=t[:], in_=b, func=AF.Relu)
                    OP("pool", "tensor_tensor", reads=[tt], writes=h_t, out=h_ap, in0=t[:], in1=t[:], op=ALU.mult)
            done(s)
        gname, bname = f"mlp_g{l}", f"mlp_b{l}"
        for m in range(DC):
            s0 = use(("down", l, m, 0))
            s1 = use(("down", l, m, 1))
            for n in range(NSUB):
                b, tb_ = next_bank()
                items = []
                for k in range(FC):
                    s = s0 if k < 16 else s1
                    h_ap, h_t = hid(k, n)
                    items.append((s.v3(16)[:, k % 16, :], h_ap, [s.tile] + h_t))
                mm_group(b, tb_, items)
                OP("dve", "scalar_tensor_tensor", reads=[tb_, t_xres[m][n]], writes=[t_xres[m][n]], out=xres[:, m, ns(n)],
                   in0=xres[:, m, ns(n)], scalar=ALPHA, in1=b, op0=ALU.mult, op1=ALU.add)
                ln_accum(xres[:, m, ns(n)], [t_xres[m][n]], m, n)
            done(s0, s1)
        for n in range(NSUB):
            ln_finish(n, [xres[:, c, ns(n)] for c in range(DC)], [[t_xres[c][n]] for c in range(DC)], ln_out_stream(gname, bname))
        sp_ = use(("proj", l))
        spv = sp_.v3(2)
        for j in range(4):
            sg = use(("gate", l, j))
            sgv = sg.v3(8)
            for mi in range(2):
                c = 2 * j + mi
                for n in range(NSUB):
                    bg, tbg = next_bank()
                    mm_group(bg, tbg, [(sgv[:, k, mi * 128:(mi + 1) * 128], xb[:, k, ns(n)], [sg.tile, t_xb[k][n]]) for k in range(DC)])
                    bp, tbp = next_bank()
                    mm_group(bp, tbp, [(spv[:, k, c * 128:(c + 1) * 128], pb16[:, k, ns(n)], [sp_.tile, t_pb16[n]]) for k in range(2)])
                    t, tt = tA()
                    OP("act", "activation", reads=[tbg], writes=[tt], out=t[:], in_=bg, func=AF.Sigmoid)
                    t2, tt2 = tB()
                    OP("dve", "tensor_tensor", reads=[tbp, tt], writes=[tt2], out=t2[:], in0=bp, in1=t[:], op=ALU.mult)
                    OP("pool", "tensor_tensor", reads=[tt2, t_xres[c][n]], writes=[t_xres[c][n]], out=xres[:, c, ns(n)],
                       in0=xres[:, c, ns(n)], in1=t2[:], op=ALU.add)
            done(sg)
        done(sp_)
        if not is_last:
            i_ = 0
            for n in range(NSUB):
                for c in range(DC):
                    e_ = ("dve", "act", "pool", "dve", "act")[i_ % 5]
                    i_ += 1
                    OP(e_, "copy" if e_ == "act" else "tensor_copy", reads=[t_xres[c][n]], writes=[t_xb[c][n]], out=xb[:, c, ns(n)],
                       in_=xres[:, c, ns(n)])

    HB_STRIDE = 3072
    ZC0 = DC * HB_STRIDE

    def hbuf(c, a, b):
        return hv_bf(c * HB_STRIDE, 1536)[:, a:b]

    def t_hbuf(c):
        return hg(c * HB_STRIDE, (c + 1) * HB_STRIDE)

    def zc(c, n):
        lo = ZC0 + c * 4096 + n * 2048
        return hv_f32(lo, NT), hg(lo, lo + 2048)

    def conv_layer(mt):
        for c in range(DC):
            if mt == 0:
                OP("pool", "memset", writes=t_hbuf(c), ap=hbuf(c, 0, 30), constant=0.0)
            else:
                OP("pool", "tensor_copy", reads=[t_halo[c]], writes=t_hbuf(c), out=hbuf(c, 0, 30), in_=halo[:, c, 0:30])
        for j in range(4):
            sa = use(("win_a", j))
            sg = use(("win_g", j))
            sav, sgv = sa.v3(8), sg.v3(8)
            for mi in range(2):
                c = 2 * j + mi
                for n in range(NSUB):
                    ba, tba = next_bank()
                    mm_group(ba, tba, [(sav[:, k, mi * 128:(mi + 1) * 128], xb[:, k, ns(n)], [sa.tile, t_xb[k][n]]) for k in range(DC)])
                    bg, tbg = next_bank()
                    mm_group(bg, tbg, [(sgv[:, k, mi * 128:(mi + 1) * 128], xb[:, k, ns(n)], [sg.tile, t_xb[k][n]]) for k in range(DC)])
                    t, tt = tA()
                    OP("act", "activation", reads=[tbg, t_const], writes=[tt], out=t[:], in_=bg, func=AF.Sigmoid, bias=vcol("b_in_g", c), scale=1.0)
                    OP("dve", "scalar_tensor_tensor", reads=[tba, tt, t_const], writes=t_hbuf(c), out=hbuf(c, 30 + n * NT, 30 + (n + 1) * NT),
                       in0=ba, scalar=vcol("b_in_a", c), in1=t[:], op0=ALU.add, op1=ALU.mult)
            done(sa, sg)
        for c in range(DC):
            di = c % 2
            wv = vecs[:, V_WDW + c * 31:V_WDW + (c + 1) * 31]
            OP("pool", "tensor_tensor", reads=[t_const], writes=[t_diag[di]], out=diag[di][:],
               in0=identf[:].unsqueeze(1).to_broadcast([128, 31, 128]), in1=wv.unsqueeze(2).to_broadcast([128, 31, 128]), op=ALU.mult)
            for n in range(NSUB):
                b, tb_ = next_bank()
                mm_group(b, tb_, [(diag[di][:, tap, :], hbuf(c, n * NT + tap, n * NT + tap + NT), [t_diag[di]] + t_hbuf(c)) for tap in range(31)])
                z_ap, z_t = zc(c, n)
                OP("act", "activation", reads=[tb_, t_const], writes=z_t, out=z_ap, in_=b, func=AF.Identity, bias=vcol("b_dw", c), scale=1.0)
                ln_accum(z_ap, z_t, c, n)
            OP("pool", "tensor_copy", reads=t_hbuf(c), writes=[t_halo[c]], out=halo[:, c, 0:30], in_=hbuf(c, TT, TT + 30))

        def conv_out(c, n, t, tt):
            OP("act", "activation", reads=[tt, t_const], writes=[t_xb[c][n]], out=xb[:, c, ns(n)], in_=t[:], func=AF.Silu,
               bias=vcol("cln_b", c), scale=vcol("cln_g", c))
        for n in range(NSUB):
            ln_finish(n, [zc(c, n)[0] for c in range(DC)], [zc(c, n)[1] for c in range(DC)], conv_out)
        proj_ln("cout", xb_rhs, "b_out", "mix_g0", "mix_b0")

    A_QT = 0
    A_OT = 16384
    A_CS = 32768
    A_SM = 40960
    A_PB = 49152
    A_PT = 53248
    A_OS = 57344

    def qT(c, a, b):
        return hv_bf(A_QT + c * 2048, TT)[:, a:b]

    def t_qT(c, n):
        lo = A_QT + c * 2048 + n * 1024
        return hg(lo, lo + 1024)

    def oT_ap(c, a, b):
        return hv_bf(A_OT + c * 2048, TT)[:, a:b]

    def t_oT(c, n):
        lo = A_OT + c * 2048 + n * 1024
        return hg(lo, lo + 1024)

    def cs_ap(which, n):
        lo = A_CS + which * 4096 + n * 2048
        return hv_f32(lo, NT)

    t_cs = hg(A_CS, A_CS + 8192) if L1 else None

    def attn_layer(mt, t0):
        P.dma("sp", ch_cs, hv_f32(A_CS, 2048).rearrange("p (a t) -> p a t", a=2),
              cs_d.rearrange("a p t -> p a t")[:, :, t0:t0 + TT], writes=t_cs)
        sk, skr, sv_ = use(("k", 0)), use(("kr", 0)), use(("v", 0))
        for kc in range(2):
            for n in range(NSUB):
                bk, tbk = next_bank()
                mm_group(bk, tbk, [(sk.v3(8)[:, k, kc * 128:(kc + 1) * 128], xb[:, k, ns(n)], [sk.tile, t_xb[k][n]]) for k in range(DC)])
                br, tbr = next_bank()
                mm_group(br, tbr, [(skr.v3(8)[:, k, kc * 128:(kc + 1) * 128], xb[:, k, ns(n)], [skr.tile, t_xb[k][n]]) for k in range(DC)])
                t1, tt1 = tA()
                t2, tt2 = tA()
                OP("dve", "tensor_tensor", reads=[tbk] + t_cs, writes=[tt1], out=t1[:], in0=bk, in1=cs_ap(0, n), op=ALU.mult)
                OP("dve", "tensor_tensor", reads=[tbr] + t_cs, writes=[tt2], out=t2[:], in0=br, in1=cs_ap(1, n), op=ALU.mult)
                for hf in range(2):
                    g = kc * 2 + hf
                    rows = slice(hf * 64, (hf + 1) * 64)
                    OP("pool", "tensor_tensor", reads=[tt1, tt2], writes=[t_kTz[g][1 + 4 * n + b4] for b4 in range(4)],
                       out=kTz[g][rows, 128 + n * NT:128 + (n + 1) * NT], in0=t1[rows, :], in1=t2[rows, :], op=ALU.add)
        svv = sv_.v3(8)
        for tb in range(8):
            b, tb_ = next_bank()
            n = tb // 4
            mm_group(b[:, 0:256], tb_, [(xb[:, k, tb * 128:(tb + 1) * 128], svv[:, k, :], [sv_.tile, t_xb[k][n]]) for k in range(DC)])
            OP("act", "copy", reads=[tb_], writes=[t_v[1 + tb]], out=v_sb[:, 1 + tb, :], in_=b[:, 0:256])
        done(sk, skr, sv_)
        for j in range(4):
            sq, sqr = use(("q", j)), use(("qr", j))
            for mi in range(2):
                c = 2 * j + mi
                for n in range(NSUB):
                    bq, tbq = next_bank()
                    mm_group(bq, tbq, [(sq.v3(8)[:, k, mi * 128:(mi + 1) * 128], xb[:, k, ns(n)], [sq.tile, t_xb[k][n]]) for k in range(DC)])
                    br, tbr = next_bank()
                    mm_group(br, tbr, [(sqr.v3(8)[:, k, mi * 128:(mi + 1) * 128], xb[:, k, ns(n)], [sqr.tile, t_xb[k][n]]) for k in range(DC)])
                    t1, tt1 = tA()
                    t2, tt2 = tA()
                    OP("dve", "tensor_tensor", reads=[tbq] + t_cs, writes=[tt1], out=t1[:], in0=bq, in1=cs_ap(0, n), op=ALU.mult)
                    OP("dve", "tensor_tensor", reads=[tbr] + t_cs, writes=[tt2], out=t2[:], in0=br, in1=cs_ap(1, n), op=ALU.mult)
                    OP("pool", "tensor_tensor", reads=[tt1, tt2], writes=t_qT(c, n), out=qT(c, n * NT, (n + 1) * NT), in0=t1[:], in1=t2[:], op=ALU.add)
            done(sq, sqr)
        PT_full = pst_[0][:, 0:512].bitcast(BF16)
        OT_full = pst_[0][:, 512:1024].bitcast(BF16)
        O_ps = pst_[1]
        def bufs(it):
            r = it % 2
            d = {}
            d["S_ps"] = pst_[2 + r][:, :].rearrange("p (j s) -> p j s", j=4)
            d["tS"] = [t_bank[4 + 2 * r], t_bank[5 + 2 * r]]
            d["sm"] = hv_f32(A_SM + r * 4096, 1024).rearrange("p (j s) -> p j s", j=4)
            d["t_sm"] = hg(A_SM + r * 4096, A_SM + r * 4096 + 4096)
            d["pbv"] = hv_bf(A_PB + r * 2048, 1024).rearrange("p (j s) -> p j s", j=4)
            d["t_pb"] = hg(A_PB + r * 2048, A_PB + r * 2048 + 2048)
            d["ptv"] = hv_bf(A_PT + r * 2048, 1024)
            d["t_pt"] = hg(A_PT + r * 2048, A_PT + r * 2048 + 2048)
            d["sx"] = stt[r]
            d["tsx"] = t_stt[r]
            return d

        def stage_a(it):
            qb, g = it // 4, it % 4
            n = qb // 4
            cbase = 4 * (g // 2)
            d = bufs(it)
            S_ps, tS, sm, t_sm, pbv, t_pb, sx, tsx = d["S_ps"], d["tS"], d["sm"], d["t_sm"], d["pbv"], d["t_pb"], d["sx"], d["tsx"]
            for j in range(4):
                c = cbase + j
                OP("pe", "matmul", reads=t_qT(c, n) + [t_kTz[g][qb], t_kTz[g][qb + 1]], writes=tS, out=S_ps[:, j, :],
                   lhsT=qT(c, qb * 128, (qb + 1) * 128), rhs=kTz[g][:, qb * 128:qb * 128 + 256], start=True, stop=True)
            mi_ = 1 if (mt == 0 and qb == 0) else 0
            OP("dve", "tensor_tensor", reads=tS + [t_const], writes=t_sm, out=sm, in0=S_ps,
               in1=msk[:, mi_, :].unsqueeze(1).to_broadcast([128, 4, 256]), op=ALU.add)
            OP("dve", "tensor_reduce", reads=t_sm, writes=[tsx], out=sx[:, 0:4], in_=sm, axis=AX.X, op=ALU.max)
            sinkg = vecs[:, V_SINK + 4 * g:V_SINK + 4 * g + 4]
            OP("dve", "scalar_tensor_tensor", reads=[tsx, t_const], writes=[tsx], out=sx[:, 4:8], in0=sx[:, 0:4], scalar=0.125, in1=sinkg,
               op0=ALU.mult, op1=ALU.max)
            OP("dve", "tensor_scalar", reads=[tsx], writes=[tsx], out=sx[:, 8:12], in0=sx[:, 4:8], scalar1=-1.0, scalar2=None, op0=ALU.mult)
            OP("dve", "tensor_tensor", reads=[tsx, t_const], writes=[tsx], out=sx[:, 16:20], in0=sinkg, in1=sx[:, 4:8], op=ALU.subtract)
            OP("dve", "memset", reads=[tsx], writes=[tsx], ap=sx[:, 12:16], constant=0.0)
            for j in range(4):
                OP("act", "activation", reads=t_sm + [tsx], writes=t_pb + [tsx], out=pbv[:, j, :], in_=sm[:, j, :], func=AF.Exp,
                   bias=sx[:, 8 + j:9 + j], scale=0.125, accum_out=sx[:, 12 + j:13 + j])
            OP("act", "activation", reads=[tsx], writes=[tsx], out=sx[:, 20:24], in_=sx[:, 16:20], func=AF.Exp)
            OP("dve", "tensor_tensor", reads=[tsx], writes=[tsx], out=sx[:, 24:28], in0=sx[:, 12:16], in1=sx[:, 20:24], op=ALU.add)
            OP("dve", "reciprocal", reads=[tsx], writes=[tsx], out=sx[:, 28:32], in_=sx[:, 24:28])

        def stage_b(it):
            qb, g = it // 4, it % 4
            n = qb // 4
            ob = qb % 2
            half = g % 2
            cbase = 4 * (g // 2)
            o_sb = hv_bf(A_OS + ob * 2048, 1024)
            t_osb = hg(A_OS + ob * 2048, A_OS + ob * 2048 + 2048)
            d = bufs(it)
            pbv, t_pb, ptv, t_pt, sx, tsx = d["pbv"], d["t_pb"], d["ptv"], d["t_pt"], d["sx"], d["tsx"]
            for j in range(4):
                for hf in range(2):
                    o_ = (j * 2 + hf) * 128
                    OP("pe", "transpose", reads=t_pb + [t_const], writes=[t_bank[0]], out=PT_full[:, o_:o_ + 128],
                       in_=pbv[:, j, hf * 128:(hf + 1) * 128], identity=identb[:])
            OP("act", "copy", reads=[t_bank[0]], writes=t_pt, out=ptv, in_=PT_full)
            for j in range(4):
                pos = (cbase + j) * 2 + half
                for hf in range(2):
                    o_ = (j * 2 + hf) * 128
                    OP("pe", "matmul", reads=t_pt + [t_v[qb + hf]], writes=[t_bank[2], t_bank[3]], out=O_ps[:, pos * 64:(pos + 1) * 64],
                       lhsT=ptv[:, o_:o_ + 128], rhs=v_sb[:, qb + hf, g * 64:(g + 1) * 64], start=(hf == 0), stop=(hf == 1))
            ov = O_ps[:, :].rearrange("p (c two e) -> p c two e", two=2, e=64)[:, cbase:cbase + 4, half, :]
            osv = o_sb.rearrange("p (c two e) -> p c two e", two=2, e=64)[:, cbase:cbase + 4, half, :]
            OP("dve", "tensor_tensor", reads=[t_bank[2], t_bank[3], tsx], writes=t_osb, out=osv, in0=ov,
               in1=sx[:, 28:32].unsqueeze(2).to_broadcast([128, 4, 64]), op=ALU.mult)
            if g == 3:
                for c in range(DC):
                    OP("pe", "transpose", reads=t_osb + [t_const], writes=[t_bank[1]], out=OT_full[:, c * 128:(c + 1) * 128],
                       in_=o_sb[:, c * 128:(c + 1) * 128], identity=identb[:])
                dst = hv_bf(A_OT, DC * TT).rearrange("p (c t) -> p c t", c=DC)[:, :, qb * 128:(qb + 1) * 128]
                OP("act", "copy", reads=[t_bank[1]], writes=[g_ for c in range(DC) for g_ in t_oT(c, n)], out=dst,
                   in_=OT_full.rearrange("p (c t) -> p c t", c=DC))

        NIT = 32
        stage_a(0)
        for it in range(NIT):
            if it + 1 < NIT:
                stage_a(it + 1)
            stage_b(it)
        for g in range(4):
            OP("pool", "tensor_copy", reads=[t_kTz[g][8]], writes=[t_kTz[g][0]], out=kTz[g][:, 0:128], in_=kTz[g][:, TT:TT + 128])
        OP("pool", "tensor_copy", reads=[t_v[8]], writes=[t_v[0]], out=v_sb[:, 0, :], in_=v_sb[:, 8, :])

        def o_rhs(k, n):
            return oT_ap(k, n * NT, (n + 1) * NT), t_oT(k, n)
        proj_ln("o", o_rhs, None, "mix_g1", "mix_b1")

    if L1:
        for g in range(4):
            OP("pool", "memset", writes=t_kTz[g], ap=kTz[g][:], constant=0.0)
        OP("pool", "memset", writes=t_v, ap=v_sb[:], constant=0.0)
    all_xres = [t_xres[c][n] for c in range(DC) for n in range(NSUB)]
    for mt in range(nmt):
        t0 = mt * TT
        P.dma("sp", ch_x, xres[:], xT.rearrange("(c p) t -> p c t", p=128)[:, :, t0:t0 + TT], writes=all_xres)
        for c in range(DC):
            for n in range(NSUB):
                eng = "pool" if (c + n) % 2 == 0 else "dve"
                OP(eng, "tensor_copy", reads=[t_xres[c][n]], writes=[t_xb[c][n]], out=xb[:, c, ns(n)], in_=xres[:, c, ns(n)])
        for l in layers:
            if l == 0:
                conv_layer(mt)
            else:
                attn_layer(mt, t0)
            mlp_ple(l, t0, l == last_layer)
        for n in range(NSUB):
            P.dma("sp", ch_out[n], outT.rearrange("(c p) t -> p c t", p=128)[:, :, t0 + n * NT:t0 + (n + 1) * NT],
                  xres[:, :, ns(n)], reads=[t_xres[c][n] for c in range(DC)], writes=[Tile("o")])
    assert st["pos"] == len(order), (st["pos"], len(order))
    P.finalize(final_waits=ch_out)
    P.close()
    return nc


def _fm(v):
    return np.ascontiguousarray(np.asarray(v, np.float32).reshape(8, 128).T)


def _rope_tables(tlen):
    pos = np.arange(tlen, dtype=np.float32)
    inv_freq = (np.float32(ROPE_THETA) ** (-np.arange(0, 16, 2, dtype=np.float32) / np.float32(16))).astype(np.float32)
    ang = (pos[:, None] * inv_freq[None, :]).astype(np.float32)
    cos, sin = np.cos(ang).astype(np.float32), np.sin(ang).astype(np.float32)
    ct = np.ones((128, tlen), np.float32)
    stb = np.zeros((128, tlen), np.float32)
    for p in range(128):
        d = p % 64
        if d < 8:
            ct[p] = cos[:, d]
            stb[p] = -sin[:, d]
        elif d < 16:
            ct[p] = cos[:, d - 8]
            stb[p] = sin[:, d - 8]
    return np.stack([ct, stb])


def _rot_cols(w, nheads):
    idx = np.arange(nheads * 64).reshape(nheads, 64).copy()
    src = idx.copy()
    src[:, 0:8] = idx[:, 8:16]
    src[:, 8:16] = idx[:, 0:8]
    return w[:, src.reshape(-1)]


def _prep(inp, layers=(0, 1)):
    f = lambda a: np.ascontiguousarray(np.asarray(a, dtype=np.float32))
    shared = {}
    if 0 in layers:
        shared["w_in"] = f(inp["conv_w_in"][0])
        shared["w_cout"] = f(inp["conv_w_out"][0])
    if 1 in layers:
        wk, wv, wq, wo = f(inp["kv_w_k"]), f(inp["kv_w_v"]), f(inp["attn_w_q"][0]), f(inp["attn_w_o"][0])
        shared["w_kk"] = np.ascontiguousarray(np.stack([wk, _rot_cols(wk, 4), wv]))
        colperm = np.concatenate([np.arange(h * 64, (h + 1) * 64) for h in PERM_HEADS])
        shared["w_qq"] = np.ascontiguousarray(np.stack([wq[:, colperm], _rot_cols(wq, 16)[:, colperm]]))
        shared["w_o"] = np.ascontiguousarray(wo[colperm, :])
        a = np.arange(128)[:, None]
        s = np.arange(256)[None, :]
        ok = (s > a) & (s <= a + 128)
        mg = np.where(ok, 0.0, NEG).astype(np.float32)
        mf = np.where(ok & (s >= 128), 0.0, NEG).astype(np.float32)
        shared["msk"] = np.ascontiguousarray(np.stack([mg, mf]))
    shared["w_up"] = f(inp["mlp_w_up"])
    shared["w_down"] = f(inp["mlp_w_down"])
    shared["w_proj"] = f(inp["ple_w_proj"])
    shared["w_gate"] = f(inp["ple_w_gate"])
    vec = np.zeros((128, NV), np.float32)

    def put(name, v):
        vec[:, VOFF[name]:VOFF[name] + 8] = _fm(v)
    b_in = f(inp["conv_b_in"][0])
    put("b_in_a", b_in[:1024]); put("b_in_g", b_in[1024:])
    put("b_dw", inp["conv_b_dw"][0]); put("cln_g", inp["conv_ln_g"][0]); put("cln_b", inp["conv_ln_b"][0])
    put("b_out", inp["conv_b_out"][0])
    for l in range(2):
        put(f"mix_g{l}", inp["mix_ln_g"][l]); put(f"mix_b{l}", inp["mix_ln_b"][l])
        put(f"mlp_g{l}", inp["mlp_ln_g"][l]); put(f"mlp_b{l}", inp["mlp_ln_b"][l])
    wdw = f(inp["conv_w_dw"][0])
    vec[:, V_WDW:V_WDW + 248] = wdw.T.reshape(8, 128, 31).transpose(1, 0, 2).reshape(128, 248)
    vec[:, V_SINK:V_SINK + 16] = np.broadcast_to(f(inp["attn_sinks"][0])[None, :], (128, 16))
    vec[:, V_EPS] = EPS
    shared["vecs"] = vec
    shared["ident"] = np.eye(128, dtype=np.float32)
    return shared


def _run(inp_x, inp_p, shared, layers, nmt=4, ncores=8):
    tlen = nmt * TT
    nc = build(layers=layers, nmt=nmt, tlen=tlen)
    if 1 in layers:
        shared = dict(shared)
        shared["cs"] = _rope_tables(tlen)
    in_maps = []
    for b in range(ncores):
        m = dict(shared)
        m["xT"] = np.ascontiguousarray(inp_x[b][:tlen].T)
        m["pT"] = np.ascontiguousarray(np.transpose(inp_p[:, b, :tlen, :], (0, 2, 1)))
        in_maps.append(m)
    res = run_bass_kernel_spmd(nc, in_maps, core_ids=list(range(ncores)))
    return [np.ascontiguousarray(r["outT"].T) for r in res.results]


def kernel(**inputs):
    x = np.asarray(inputs["x"], np.float32)
    p = np.asarray(inputs["p"], np.float32)
    shared = _prep(inputs, (0, 1))
    outs = _run(x, p, shared, (0, 1), nmt=4, ncores=8)
    return np.stack(outs).astype(np.float32)
```

```python
import numpy as np
import concourse.bass as bass
import concourse.mybir as mybir
from concourse.bass_utils import run_bass_kernel_spmd

F32 = mybir.dt.float32
BF16 = mybir.dt.bfloat16
AF = mybir.ActivationFunctionType
ALU = mybir.AluOpType
AX = mybir.AxisListType

D = 1024
T = 4096
TT = 1024
NT = 512
NSUB = 2
DC = 8
FC = 32
NHEAD = 16
ALPHA = float(4.0 ** 0.25)
EPS = 1e-5
NEG = -30000.0
ROPE_THETA = 500000.0
PERM_HEADS = [0, 4, 1, 5, 2, 6, 3, 7, 8, 12, 9, 13, 10, 14, 11, 15]

ENGS = ("pe", "act", "dve", "pool", "sp")
STRICT_SAME_ENGINE = True


class Tile:
    __slots__ = ("name", "lw", "rd")

    def __init__(self, name):
        self.name = name
        self.lw = None
        self.rd = {}


class Op:
    __slots__ = ("eng", "fn", "deps", "marked", "seq", "is_dma", "chan", "dcount")

    def __init__(self, eng, fn, is_dma=False):
        self.eng = eng
        self.fn = fn
        self.deps = []
        self.marked = False
        self.seq = None
        self.is_dma = is_dma
        self.chan = None
        self.dcount = None


class Chan:
    def __init__(self, sem, name):
        self.sem = sem
        self.count = 0
        self.name = name


class Prog:
    def __init__(self, nc):
        self.nc = nc
        self.ops = []
        self.q = {e: [] for e in ENGS}
        self.sems = {}
        self._ctx = []

    def enter(self, cm):
        v = cm.__enter__()
        self._ctx.append(cm)
        return v

    def sbuf(self, name, shape, dtype):
        return self.enter(self.nc.sbuf_tensor("sb_" + name, list(shape), dtype))

    def psum(self, name, shape, dtype):
        return self.enter(self.nc.psum_tensor("pp_" + name, list(shape), dtype))

    def chan(self, name):
        s = self.enter(self.nc.semaphore("c_" + name))
        return Chan(s, name)

    def close(self):
        for cm in reversed(self._ctx):
            cm.__exit__(None, None, None)
        self._ctx = []

    def _track(self, op, reads, writes):
        deps = op.deps
        for t in reads:
            if t.lw is not None:
                deps.append(("raw", t.lw))
        for t in writes:
            if t.lw is not None:
                deps.append(("waw", t.lw))
            for r in t.rd.values():
                if isinstance(r, list):
                    for x in r:
                        deps.append(("war", x))
                else:
                    deps.append(("war", r))
        for t in reads:
            if op.is_dma:
                t.rd.setdefault("dma", []).append(op)
            else:
                t.rd[op.eng] = op
        for t in writes:
            t.lw = op
            t.rd = {}

    def op(self, eng, fn, reads=(), writes=()):
        o = Op(eng, fn)
        self._track(o, reads, writes)
        self.ops.append(o)
        self.q[eng].append(o)
        return o

    def dma(self, eng, chan, out, in_, reads=(), writes=()):
        def fn(e, out=out, in_=in_):
            return e.dma_start(out=out, in_=in_)
        o = Op(eng, fn, is_dma=True)
        o.chan = chan
        chan.count += 16
        o.dcount = chan.count
        self._track(o, reads, writes)
        self.ops.append(o)
        self.q[eng].append(o)
        return o

    def finalize(self, final_waits=()):
        nc = self.nc
        for e in ENGS:
            self.sems[e] = self.enter(nc.semaphore("p_" + e))
        for o in self.ops:
            real = []
            for kind, d in o.deps:
                if d.is_dma:
                    real.append(d)
                    continue
                if d.eng == o.eng and not o.is_dma:
                    if d.eng == "pe":
                        continue
                    if kind != "raw" and not STRICT_SAME_ENGINE:
                        continue
                d.marked = True
                real.append(d)
            o.deps = real
        cnt = {e: 0 for e in ENGS}
        for e in ENGS:
            for o in self.q[e]:
                if o.marked and not o.is_dma:
                    cnt[e] += 1
                    o.seq = cnt[e]
        known = {e: {} for e in ENGS}
        emit = {e: [] for e in ENGS}
        nw = 0
        for o in self.ops:
            need = {}
            for d in o.deps:
                if d.is_dma:
                    key = ("c", id(d.chan))
                    sem, tgt = d.chan.sem, d.dcount
                else:
                    key = ("e", d.eng)
                    sem, tgt = self.sems[d.eng], d.seq
                if known[o.eng].get(key, 0) >= tgt:
                    continue
                if key not in need or need[key][1] < tgt:
                    need[key] = (sem, tgt)
            for key, (sem, tgt) in need.items():
                known[o.eng][key] = tgt
                emit[o.eng].append(("wait", sem, tgt))
                nw += 1
            emit[o.eng].append(("op", o))
        self.nwaits = nw
        self.counts = cnt
        sems = self.sems

        def run(engobj, ekey):
            for it in emit[ekey]:
                if it[0] == "wait":
                    engobj.wait_ge(it[1], it[2])
                else:
                    o = it[1]
                    ins = o.fn(engobj)
                    if o.is_dma:
                        ins.then_inc(o.chan.sem, 16)
                    elif o.marked:
                        ins.then_inc(sems[ekey], 1)
            if ekey == "sp":
                for ch in final_waits:
                    engobj.wait_ge(ch.sem, ch.count)

        with nc.Block() as block:
            @block.tensor
            def _(e):
                run(e, "pe")

            @block.scalar
            def _(e):
                run(e, "act")

            @block.vector
            def _(e):
                run(e, "dve")

            @block.gpsimd
            def _(e):
                run(e, "pool")

            @block.sync
            def _(e):
                run(e, "sp")


VEC_NAMES = ["b_in_a", "b_in_g", "b_dw", "cln_g", "cln_b", "b_out",
             "mix_g0", "mix_b0", "mlp_g0", "mlp_b0", "mix_g1", "mix_b1", "mlp_g1", "mlp_b1"]
VOFF = {n: i * 8 for i, n in enumerate(VEC_NAMES)}
V_WDW = len(VEC_NAMES) * 8
V_SINK = V_WDW + 8 * 31
V_EPS = V_SINK + 16
NV = V_EPS + 1


def _unit_table(layers):
    units = []
    idx = {}

    def add(key, kind, arg):
        idx[key] = len(units)
        units.append((key, kind, arg))

    if 0 in layers:
        for j in range(4):
            add(("win_a", j), "A", ("w_in", None, j * 256))
            add(("win_g", j), "A", ("w_in", None, 1024 + j * 256))
        for j in range(4):
            add(("cout", j), "A", ("w_cout", None, j * 256))
    if 1 in layers:
        for i, nm in enumerate(("k", "kr", "v")):
            add((nm, 0), "A", ("w_kk", i, 0))
        for j in range(4):
            add(("q", j), "A", ("w_qq", 0, j * 256))
            add(("qr", j), "A", ("w_qq", 1, j * 256))
        for j in range(4):
            add(("o", j), "A", ("w_o", None, j * 256))
    for l in layers:
        for j in range(16):
            add(("up", l, j), "A", ("w_up", l, j * 256))
        for m in range(8):
            for h in range(2):
                add(("down", l, m, h), "DN", ("w_down", l, m, h))
        for j in range(4):
            add(("gate", l, j), "A", ("w_gate", l, j * 256))
        add(("proj", l), "PJ", ("w_proj", l))
    return units, idx


def _layer_order(idx, l):
    o = []
    if l == 0:
        for j in range(4):
            o += [idx[("win_a", j)], idx[("win_g", j)]]
        o += [idx[("cout", j)] for j in range(4)]
    else:
        o += [idx[("k", 0)], idx[("kr", 0)], idx[("v", 0)]]
        for j in range(4):
            o += [idx[("q", j)], idx[("qr", j)]]
        o += [idx[("o", j)] for j in range(4)]
    o += [idx[("up", l, j)] for j in range(16)]
    for m in range(8):
        o += [idx[("down", l, m, 0)], idx[("down", l, m, 1)]]
    o.append(idx[("proj", l)])
    o += [idx[("gate", l, j)] for j in range(4)]
    return o


class Slot:
    __slots__ = ("ap", "tile", "idx")

    def __init__(self, ap, tile, idx):
        self.ap = ap
        self.tile = tile
        self.idx = idx

    def v3(self, kc):
        return self.ap[:, :].rearrange("p (k m) -> p k m", k=kc)


def build(layers=(0, 1), nmt=4, tlen=T):
    nc = bass.Bass("TRN2", target_bir_lowering=False)
    P = Prog(nc)
    L0 = 0 in layers
    L1 = 1 in layers
    last_layer = layers[-1]

    def din(name, shape):
        return nc.dram_tensor(name, list(shape), F32, kind="ExternalInput").ap()

    xT = din("xT", [D, tlen])
    pT = din("pT", [2, 256, tlen])
    W = {}
    if L0:
        W["w_in"] = din("w_in", [D, 2048])
        W["w_cout"] = din("w_cout", [D, D])
    if L1:
        W["w_kk"] = din("w_kk", [3, D, 256])
        W["w_qq"] = din("w_qq", [2, D, D])
        W["w_o"] = din("w_o", [D, D])
        cs_d = din("cs", [2, 128, tlen])
        msk_d = din("msk", [2, 128, 256])
    W["w_up"] = din("w_up", [2, D, 4096])
    W["w_down"] = din("w_down", [2, 4096, D])
    W["w_proj"] = din("w_proj", [2, 256, D])
    W["w_gate"] = din("w_gate", [2, D, D])
    vecs_d = din("vecs", [128, NV])
    ident_d = din("ident", [128, 128])
    outT = nc.dram_tensor("outT", [D, tlen], F32, kind="ExternalOutput").ap()

    units, uidx = _unit_table(layers)
    NU = len(units)
    wsc = nc.dram_tensor("wsc", [NU, 128, 2048], BF16).ap()
    t_wsc = [Tile(f"wsc{u}") for u in range(NU)]

    def unit_src(u):
        key, kind, arg = units[u]
        if kind == "A":
            nm, li, c0 = arg
            w = W[nm] if li is None else W[nm][li]
            return w.rearrange("(k p) n -> p k n", p=128)[:, :, c0:c0 + 256], 8
        if kind == "DN":
            nm, li, m, h = arg
            w = W[nm][li]
            return w.rearrange("(k p) n -> p k n", p=128)[:, h * 16:(h + 1) * 16, m * 128:(m + 1) * 128], 16
        nm, li = arg
        w = W[nm][li]
        return w.rearrange("(k p) n -> p k n", p=128)[:, :, :], 2

    xres = P.sbuf("xres", [128, DC, TT], F32)
    xb = P.sbuf("xb", [128, DC, TT], BF16)
    HB = 66560
    Hs = P.sbuf("Hs", [128, HB // 2], BF16)
    NG = HB // 1024
    tH = [Tile(f"H{g}") for g in range(NG)]

    def hg(lo, hi):
        return tH[lo // 1024:(hi + 1023) // 1024]

    def hv_bf(lo, n):
        return Hs[:, lo // 2:lo // 2 + n]

    def hv_f32(lo, n):
        return Hs[:, lo // 2:lo // 2 + 2 * n].bitcast(F32)

    vecs = P.sbuf("vecs", [128, NV], F32)
    identf = P.sbuf("identf", [128, 128], F32)
    identb = P.sbuf("identb", [128, 128], BF16)
    onesD = P.sbuf("onesD", [128, 128], BF16)
    t_const = Tile("const")
    tmpA = [P.sbuf(f"tmpA{i}", [128, NT], F32) for i in range(4)]
    t_tmpA = [Tile(f"tmpA{i}") for i in range(4)]
    tmpB = [P.sbuf(f"tmpB{i}", [128, NT], F32) for i in range(2)]
    t_tmpB = [Tile(f"tmpB{i}") for i in range(2)]
    zbb = [P.sbuf(f"zb{i}", [128, NT], BF16) for i in range(2)]
    zsb = [P.sbuf(f"zs{i}", [128, NT], BF16) for i in range(2)]
    t_zbb = [Tile(f"zb{i}") for i in range(2)]
    t_zsb = [Tile(f"zs{i}") for i in range(2)]
    mean_sb = [P.sbuf(f"mean{i}", [128, NT], F32) for i in range(2)]
    rstd_sb = [P.sbuf(f"rstd{i}", [128, NT], F32) for i in range(2)]
    t_mean = [Tile(f"mean{i}") for i in range(2)]
    t_rstd = [Tile(f"rstd{i}") for i in range(2)]
    pst = P.sbuf("pst", [128, 2, NT], F32)
    t_pst = Tile("pst")
    pb16 = P.sbuf("pb16", [128, 2, TT], BF16)
    t_pb16 = [Tile(f"pb16_{n}") for n in range(NSUB)]
    if L0:
        diag = [P.sbuf(f"diag{i}", [128, 31, 128], BF16) for i in range(2)]
        t_diag = [Tile(f"diag{i}") for i in range(2)]
        halo = P.sbuf("halo", [128, DC, 32], BF16)
        t_halo = [Tile(f"halo{c}") for c in range(DC)]
    if L1:
        kTz = [P.sbuf(f"kTz{g}", [128, 128 + TT], BF16) for g in range(4)]
        t_kTz = [[Tile(f"kTz{g}_{b}") for b in range(9)] for g in range(4)]
        v_sb = P.sbuf("v_sb", [128, 9, 256], BF16)
        t_v = [Tile(f"v{b}") for b in range(9)]
        msk = P.sbuf("msk", [128, 2, 256], F32)
        stt = [P.sbuf(f"stt{i}", [128, 32], F32) for i in range(2)]
        t_stt = [Tile(f"stt{i}") for i in range(2)]
        stt3 = [P.sbuf(f"sttb{i}", [128, 32], F32) for i in range(3)]
        t_stt3 = [Tile(f"sttb{i}") for i in range(3)]

    t_xres = [[Tile(f"xres{c}_{n}") for n in range(NSUB)] for c in range(DC)]
    t_xb = [[Tile(f"xb{c}_{n}") for n in range(NSUB)] for c in range(DC)]

    pst_ = [P.psum(f"ps{i}", [128, 1024], F32) for i in range(4)]
    t_bank = [Tile(f"bank{i}") for i in range(8)]

    def bank_ap(i):
        return pst_[i // 2][:, (i % 2) * 512:(i % 2) * 512 + 512]

    gbi = [0]

    def next_bank():
        i = gbi[0] % 4
        gbi[0] += 1
        return bank_ap(i), t_bank[i]

    def OP(eng, meth, reads=(), writes=(), **kw):
        return P.op(eng, lambda e: getattr(e, meth)(**kw), reads=reads, writes=writes)

    def vcol(name, c):
        o = VOFF[name] + c
        return vecs[:, o:o + 1]

    ch_x = P.chan("x")
    ch_out = [P.chan(f"out{n}") for n in range(NSUB)]
    ch_c = P.chan("consts")
    ch_p = P.chan("p")
    ch_cs = P.chan("cs")

    P.dma("sp", ch_c, vecs[:], vecs_d[:, :], writes=[t_const])
    P.dma("sp", ch_c, identf[:], ident_d[:, :], writes=[t_const])
    if L1:
        P.dma("sp", ch_c, msk[:], msk_d.rearrange("a p s -> p a s"), writes=[t_const])
    OP("dve", "tensor_copy", reads=[t_const], writes=[t_const], out=identb[:], in_=identf[:])
    OP("dve", "memset", writes=[t_const], ap=onesD[:], constant=1.0 / D)

    bars = {e: Tile("bar_" + e) for e in ENGS}
    barl = list(bars.values())
    NSTG = 4
    stg_f = [hv_f32(i * 8192, 2048) for i in range(NSTG)]
    stg_b = [hv_bf(32768 + i * 4096, 2048) for i in range(NSTG)]
    t_sf = [Tile(f"sf{i}") for i in range(NSTG)]
    t_sb = [Tile(f"sb{i}") for i in range(NSTG)]
    ch_sf = [P.chan(f"sf{i}") for i in range(NSTG)]
    ch_sb = [P.chan(f"sb{i}") for i in range(NSTG)]
    order1 = []
    for l in layers:
        order1 += _layer_order(uidx, l)
    assert sorted(order1) == list(range(NU))
    cast_eng = ("dve", "act", "pool")
    for i, u in enumerate(order1):
        s = i % NSTG
        src, kc = unit_src(u)
        P.dma("sp", ch_sf[s], stg_f[s].rearrange("p (k m) -> p k m", k=kc), src, reads=barl, writes=[t_sf[s]])
        ce = cast_eng[i % 3]
        if ce == "act":
            OP("act", "copy", reads=[t_sf[s]] + barl, writes=[t_sb[s]], out=stg_b[s], in_=stg_f[s])
        else:
            OP(ce, "tensor_copy", reads=[t_sf[s]] + barl, writes=[t_sb[s]], out=stg_b[s], in_=stg_f[s])
        P.dma("act", ch_sb[s], wsc[u, :, :], stg_b[s], reads=[t_sb[s]] + barl, writes=[t_wsc[u]])
    for e in ENGS:
        P.op(e, lambda en: en.nop(), writes=[bars[e]])

    NSLOT = 7
    ws_slots = [P.sbuf(f"ws{i}", [128, 2048], BF16) for i in range(NSLOT)]
    ws_tiles = [Tile(f"ws{i}") for i in range(NSLOT)]
    ws_ch = [P.chan(f"ws{i}") for i in range(NSLOT)]
    order = order1 * nmt
    st = {"issued": 0, "pos": 0, "rel": [True] * NSLOT}

    def pump():
        while st["issued"] < len(order):
            s = st["issued"] % NSLOT
            if not st["rel"][s]:
                break
            u = order[st["issued"]]
            P.dma("sp", ws_ch[s], ws_slots[s][:], wsc[u, :, :], reads=[t_wsc[u]], writes=[ws_tiles[s]])
            st["rel"][s] = False
            st["issued"] += 1

    def use(key):
        u = uidx[key]
        assert order[st["pos"]] == u, (key, st["pos"])
        pump()
        assert st["issued"] > st["pos"]
        s = st["pos"] % NSLOT
        st["pos"] += 1
        return Slot(ws_slots[s], ws_tiles[s], s)

    def done(*slots):
        for sl in slots:
            st["rel"][sl.idx] = True
        pump()

    def ns(n):
        return slice(n * NT, (n + 1) * NT)

    tAi = [0]
    tBi = [0]
    zi = [0]

    def tA():
        i = tAi[0] % 4
        tAi[0] += 1
        return tmpA[i], t_tmpA[i]

    def tB():
        i = tBi[0] % 2
        tBi[0] += 1
        return tmpB[i], t_tmpB[i]

    def mm_group(out_ap, out_tile, items):
        nI = len(items)
        for i, (l, r, rd) in enumerate(items):
            OP("pe", "matmul", reads=rd, writes=[out_tile], out=out_ap, lhsT=l, rhs=r, start=(i == 0), stop=(i == nI - 1))

    def stat_banks(n):
        return (bank_ap(4 + 2 * n), t_bank[4 + 2 * n]), (bank_ap(5 + 2 * n), t_bank[5 + 2 * n])

    def ln_accum(z_ap, z_tiles, c, n):
        (mb, tmb), (qb_, tqb) = stat_banks(n)
        i = zi[0] % 2
        zi[0] += 1
        OP("act", "copy", reads=z_tiles, writes=[t_zbb[i]], out=zbb[i][:], in_=z_ap)
        OP("act", "activation", reads=z_tiles, writes=[t_zsb[i]], out=zsb[i][:], in_=z_ap, func=AF.Square)
        OP("pe", "matmul", reads=[t_zbb[i], t_const], writes=[tmb], out=mb, lhsT=onesD[:], rhs=zbb[i][:], start=(c == 0), stop=(c == DC - 1))
        OP("pe", "matmul", reads=[t_zsb[i], t_const], writes=[tqb], out=qb_, lhsT=onesD[:], rhs=zsb[i][:], start=(c == 0), stop=(c == DC - 1))

    LN_POOL_CHUNKS = (2, 5, 7)

    def ln_finish(n, z_aps, z_tiles, emit_out):
        (mb, tmb), (qb_, tqb) = stat_banks(n)
        OP("act", "copy", reads=[tmb], writes=[t_mean[n]], out=mean_sb[n][:], in_=mb)
        OP("act", "activation", reads=[tmb], writes=[t_rstd[n]], out=rstd_sb[n][:], in_=mb, func=AF.Square)
        OP("dve", "tensor_tensor", reads=[tqb, t_rstd[n]], writes=[t_rstd[n]], out=rstd_sb[n][:], in0=qb_, in1=rstd_sb[n][:], op=ALU.subtract)
        OP("act", "activation", reads=[t_rstd[n], t_const], writes=[t_rstd[n]], out=rstd_sb[n][:], in_=rstd_sb[n][:], func=AF.Sqrt,
           bias=vecs[:, V_EPS:V_EPS + 1], scale=1.0)
        OP("dve", "reciprocal", reads=[t_rstd[n]], writes=[t_rstd[n]], out=rstd_sb[n][:], in_=rstd_sb[n][:])
        for c in range(DC):
            t, tt = tA()
            eng = "pool" if c in LN_POOL_CHUNKS else "dve"
            OP(eng, "tensor_tensor", reads=z_tiles[c] + [t_mean[n]], writes=[tt], out=t[:], in0=z_aps[c], in1=mean_sb[n][:], op=ALU.subtract)
            OP(eng, "tensor_tensor", reads=[tt, t_rstd[n]], writes=[tt], out=t[:], in0=t[:], in1=rstd_sb[n][:], op=ALU.mult)
            emit_out(c, n, t, tt)

    def ln_out_stream(gname, bname):
        def f(c, n, t, tt):
            OP("act", "activation", reads=[tt, t_const], writes=[t_xb[c][n]], out=xb[:, c, ns(n)], in_=t[:], func=AF.Identity,
               bias=vcol(bname, c), scale=vcol(gname, c))
            OP("act", "activation", reads=[tt, t_const], writes=[t_xres[c][n]], out=xres[:, c, ns(n)], in_=t[:], func=AF.Identity,
               bias=vcol(bname, c), scale=vcol(gname, c))
        return f

    def proj_ln(ukey, rhs_fn, bias_name, gname, bname):
        for j in range(4):
            s = use((ukey, j))
            sv = s.v3(8)
            for mi in range(2):
                c = 2 * j + mi
                for n in range(NSUB):
                    b, tb_ = next_bank()
                    items = []
                    for k in range(DC):
                        r, rt = rhs_fn(k, n)
                        items.append((sv[:, k, mi * 128:(mi + 1) * 128], r, [s.tile] + rt))
                    if bias_name is not None:
                        OP("pool", "tensor_scalar", reads=[t_xres[c][n], t_const], writes=[t_xres[c][n]], out=xres[:, c, ns(n)],
                           in0=xres[:, c, ns(n)], scalar1=ALPHA, scalar2=vcol(bias_name, c), op0=ALU.mult, op1=ALU.add)
                    mm_group(b, tb_, items)
                    if bias_name is not None:
                        OP("dve", "tensor_tensor", reads=[tb_, t_xres[c][n]], writes=[t_xres[c][n]], out=xres[:, c, ns(n)],
                           in0=b, in1=xres[:, c, ns(n)], op=ALU.add)
                    else:
                        OP("dve", "scalar_tensor_tensor", reads=[tb_, t_xres[c][n]], writes=[t_xres[c][n]], out=xres[:, c, ns(n)],
                           in0=xres[:, c, ns(n)], scalar=ALPHA, in1=b, op0=ALU.mult, op1=ALU.add)
                    ln_accum(xres[:, c, ns(n)], [t_xres[c][n]], c, n)
            done(s)
        for n in range(NSUB):
            ln_finish(n, [xres[:, c, ns(n)] for c in range(DC)], [[t_xres[c][n]] for c in range(DC)], ln_out_stream(gname, bname))

    def xb_rhs(k, n):
        return xb[:, k, ns(n)], [t_xb[k][n]]

    def hid(m, n):
        lo = m * 2048 + n * 1024
        return hv_bf(lo, NT), hg(lo, lo + 1024)

    def load_p(l, t0):
        for n in range(NSUB):
            P.dma("sp", ch_p, pst[:], pT[l].rearrange("(k p) t -> p k t", p=128)[:, :, t0 + n * NT:t0 + (n + 1) * NT],
                  writes=[t_pst])
            OP("pool", "tensor_copy", reads=[t_pst], writes=[t_pb16[n]], out=pb16[:, :, ns(n)], in_=pst[:])

    def mlp_ple(l, t0, is_last):
        load_p(l, t0)
        for j in range(16):
            s = use(("up", l, j))
            sv = s.v3(8)
            for mi in range(2):
                m = 2 * j + mi
                for n in range(NSUB):
                    b, tb_ = next_bank()
                    mm_group(b, tb_, [(sv[:, k, mi * 128:(mi + 1) * 128], xb[:, k, ns(n)], [s.tile, t_xb[k][n]]) for k in range(DC)])
                    t, tt = tA()
                    h_ap, h_t = hid(m, n)
                    OP("act", "activation", reads=[tb_], writes=[tt], out=t[:], in_=b, func=AF.Relu)
                    OP("pool", "tensor_tensor", reads=[tt], writes=h_t, out=h_ap, in0=t[:], in1=t[:], op=ALU.mult)
            done(s)
        gname, bname = f"mlp_g{l}", f"mlp_b{l}"
        for m in range(DC):
            s0 = use(("down", l, m, 0))
            s1 = use(("down", l, m, 1))
            for n in range(NSUB):
                b, tb_ = next_bank()
                items = []
                for k in range(FC):
                    s = s0 if k < 16 else s1
                    h_ap, h_t = hid(k, n)
                    items.append((s.v3(16)[:, k % 16, :], h_ap, [s.tile] + h_t))
                mm_group(b, tb_, items)
                OP("dve", "scalar_tensor_tensor", reads=[tb_, t_xres[m][n]], writes=[t_xres[m][n]], out=xres[:, m, ns(n)],
                   in0=xres[:, m, ns(n)], scalar=ALPHA, in1=b, op0=ALU.mult, op1=ALU.add)
                ln_accum(xres[:, m, ns(n)], [t_xres[m][n]], m, n)
            done(s0, s1)
        for n in range(NSUB):
            ln_finish(n, [xres[:, c, ns(n)] for c in range(DC)], [[t_xres[c][n]] for c in range(DC)], ln_out_stream(gname, bname))
        sp_ = use(("proj", l))
        spv = sp_.v3(2)
        for j in range(4):
            sg = use(("gate", l, j))
            sgv = sg.v3(8)
            for mi in range(2):
                c = 2 * j + mi
                for n in range(NSUB):
                    bg, tbg = next_bank()
                    mm_group(bg, tbg, [(sgv[:, k, mi * 128:(mi + 1) * 128], xb[:, k, ns(n)], [sg.tile, t_xb[k][n]]) for k in range(DC)])
                    bp, tbp = next_bank()
                    mm_group(bp, tbp, [(spv[:, k, c * 128:(c + 1) * 128], pb16[:, k, ns(n)], [sp_.tile, t_pb16[n]]) for k in range(2)])
                    t, tt = tA()
                    OP("act", "activation", reads=[tbg], writes=[tt], out=t[:], in_=bg, func=AF.Sigmoid)
                    t2, tt2 = tB()
                    OP("dve", "tensor_tensor", reads=[tbp, tt], writes=[tt2], out=t2[:], in0=bp, in1=t[:], op=ALU.mult)
                    OP("pool", "tensor_tensor", reads=[tt2, t_xres[c][n]], writes=[t_xres[c][n]], out=xres[:, c, ns(n)],
                       in0=xres[:, c, ns(n)], in1=t2[:], op=ALU.add)
            done(sg)
        done(sp_)
        if not is_last:
            i_ = 0
            for n in range(NSUB):
                for c in range(DC):
                    e_ = ("dve", "act", "pool", "dve", "act")[i_ % 5]
                    i_ += 1
                    OP(e_, "copy" if e_ == "act" else "tensor_copy", reads=[t_xres[c][n]], writes=[t_xb[c][n]], out=xb[:, c, ns(n)],
                       in_=xres[:, c, ns(n)])

    HB_STRIDE = 3072
    ZC0 = DC * HB_STRIDE

    def hbuf(c, a, b):
        return hv_bf(c * HB_STRIDE, 1536)[:, a:b]

    def t_hbuf(c):
        return hg(c * HB_STRIDE, (c + 1) * HB_STRIDE)

    def zc(c, n):
        lo = ZC0 + c * 4096 + n * 2048
        return hv_f32(lo, NT), hg(lo, lo + 2048)

    def conv_layer(mt):
        for c in range(DC):
            if mt == 0:
                OP("pool", "memset", writes=t_hbuf(c), ap=hbuf(c, 0, 30), constant=0.0)
            else:
                OP("pool", "tensor_copy", reads=[t_halo[c]], writes=t_hbuf(c), out=hbuf(c, 0, 30), in_=halo[:, c, 0:30])
        for j in range(4):
            sa = use(("win_a", j))
            sg = use(("win_g", j))
            sav, sgv = sa.v3(8), sg.v3(8)
            for mi in range(2):
                c = 2 * j + mi
                for n in range(NSUB):
                    ba, tba = next_bank()
                    mm_group(ba, tba, [(sav[:, k, mi * 128:(mi + 1) * 128], xb[:, k, ns(n)], [sa.tile, t_xb[k][n]]) for k in range(DC)])
                    bg, tbg = next_bank()
                    mm_group(bg, tbg, [(sgv[:, k, mi * 128:(mi + 1) * 128], xb[:, k, ns(n)], [sg.tile, t_xb[k][n]]) for k in range(DC)])
                    t, tt = tA()
                    OP("act", "activation", reads=[tbg, t_const], writes=[tt], out=t[:], in_=bg, func=AF.Sigmoid, bias=vcol("b_in_g", c), scale=1.0)
                    OP("dve", "scalar_tensor_tensor", reads=[tba, tt, t_const], writes=t_hbuf(c), out=hbuf(c, 30 + n * NT, 30 + (n + 1) * NT),
                       in0=ba, scalar=vcol("b_in_a", c), in1=t[:], op0=ALU.add, op1=ALU.mult)
            done(sa, sg)
        for c in range(DC):
            di = c % 2
            wv = vecs[:, V_WDW + c * 31:V_WDW + (c + 1) * 31]
            OP("pool", "tensor_tensor", reads=[t_const], writes=[t_diag[di]], out=diag[di][:],
               in0=identf[:].unsqueeze(1).to_broadcast([128, 31, 128]), in1=wv.unsqueeze(2).to_broadcast([128, 31, 128]), op=ALU.mult)
            for n in range(NSUB):
                b, tb_ = next_bank()
                mm_group(b, tb_, [(diag[di][:, tap, :], hbuf(c, n * NT + tap, n * NT + tap + NT), [t_diag[di]] + t_hbuf(c)) for tap in range(31)])
                z_ap, z_t = zc(c, n)
                OP("act", "activation", reads=[tb_, t_const], writes=z_t, out=z_ap, in_=b, func=AF.Identity, bias=vcol("b_dw", c), scale=1.0)
                ln_accum(z_ap, z_t, c, n)
            OP("pool", "tensor_copy", reads=t_hbuf(c), writes=[t_halo[c]], out=halo[:, c, 0:30], in_=hbuf(c, TT, TT + 30))

        def conv_out(c, n, t, tt):
            OP("act", "activation", reads=[tt, t_const], writes=[t_xb[c][n]], out=xb[:, c, ns(n)], in_=t[:], func=AF.Silu,
               bias=vcol("cln_b", c), scale=vcol("cln_g", c))
        for n in range(NSUB):
            ln_finish(n, [zc(c, n)[0] for c in range(DC)], [zc(c, n)[1] for c in range(DC)], conv_out)
        proj_ln("cout", xb_rhs, "b_out", "mix_g0", "mix_b0")

    A_QT = 0
    A_OT = 16384
    A_CS = 32768
    A_SM = 40960
    A_PB = 49152
    A_PT = 53248
    A_OS = 57344

    def qT(c, a, b):
        return hv_bf(A_QT + c * 2048, TT)[:, a:b]

    def t_qT(c, n):
        lo = A_QT + c * 2048 + n * 1024
        return hg(lo, lo + 1024)

    def oT_ap(c, a, b):
        return hv_bf(A_OT + c * 2048, TT)[:, a:b]

    def t_oT(c, n):
        lo = A_OT + c * 2048 + n * 1024
        return hg(lo, lo + 1024)

    def cs_ap(which, n):
        lo = A_CS + which * 4096 + n * 2048
        return hv_f32(lo, NT)

    t_cs = hg(A_CS, A_CS + 8192) if L1 else None

    def attn_layer(mt, t0):
        P.dma("sp", ch_cs, hv_f32(A_CS, 2048).rearrange("p (a t) -> p a t", a=2),
              cs_d.rearrange("a p t -> p a t")[:, :, t0:t0 + TT], writes=t_cs)
        sk, skr, sv_ = use(("k", 0)), use(("kr", 0)), use(("v", 0))
        for kc in range(2):
            for n in range(NSUB):
                bk, tbk = next_bank()
                mm_group(bk, tbk, [(sk.v3(8)[:, k, kc * 128:(kc + 1) * 128], xb[:, k, ns(n)], [sk.tile, t_xb[k][n]]) for k in range(DC)])
                br, tbr = next_bank()
                mm_group(br, tbr, [(skr.v3(8)[:, k, kc * 128:(kc + 1) * 128], xb[:, k, ns(n)], [skr.tile, t_xb[k][n]]) for k in range(DC)])
                t1, tt1 = tA()
                t2, tt2 = tA()
                OP("dve", "tensor_tensor", reads=[tbk] + t_cs, writes=[tt1], out=t1[:], in0=bk, in1=cs_ap(0, n), op=ALU.mult)
                OP("dve", "tensor_tensor", reads=[tbr] + t_cs, writes=[tt2], out=t2[:], in0=br, in1=cs_ap(1, n), op=ALU.mult)
                for hf in range(2):
                    g = kc * 2 + hf
                    rows = slice(hf * 64, (hf + 1) * 64)
                    OP("pool", "tensor_tensor", reads=[tt1, tt2], writes=[t_kTz[g][1 + 4 * n + b4] for b4 in range(4)],
                       out=kTz[g][rows, 128 + n * NT:128 + (n + 1) * NT], in0=t1[rows, :], in1=t2[rows, :], op=ALU.add)
        svv = sv_.v3(8)
        for tb in range(8):
            b, tb_ = next_bank()
            n = tb // 4
            mm_group(b[:, 0:256], tb_, [(xb[:, k, tb * 128:(tb + 1) * 128], svv[:, k, :], [sv_.tile, t_xb[k][n]]) for k in range(DC)])
            OP("act", "copy", reads=[tb_], writes=[t_v[1 + tb]], out=v_sb[:, 1 + tb, :], in_=b[:, 0:256])
        done(sk, skr, sv_)
        for j in range(4):
            sq, sqr = use(("q", j)), use(("qr", j))
            for mi in range(2):
                c = 2 * j + mi
                for n in range(NSUB):
                    bq, tbq = next_bank()
                    mm_group(bq, tbq, [(sq.v3(8)[:, k, mi * 128:(mi + 1) * 128], xb[:, k, ns(n)], [sq.tile, t_xb[k][n]]) for k in range(DC)])
                    br, tbr = next_bank()
                    mm_group(br, tbr, [(sqr.v3(8)[:, k, mi * 128:(mi + 1) * 128], xb[:, k, ns(n)], [sqr.tile, t_xb[k][n]]) for k in range(DC)])
                    t1, tt1 = tA()
                    t2, tt2 = tA()
                    OP("dve", "tensor_tensor", reads=[tbq] + t_cs, writes=[tt1], out=t1[:], in0=bq, in1=cs_ap(0, n), op=ALU.mult)
                    OP("dve", "tensor_tensor", reads=[tbr] + t_cs, writes=[tt2], out=t2[:], in0=br, in1=cs_ap(1, n), op=ALU.mult)
                    OP("pool", "tensor_tensor", reads=[tt1, tt2], writes=t_qT(c, n), out=qT(c, n * NT, (n + 1) * NT), in0=t1[:], in1=t2[:], op=ALU.add)
            done(sq, sqr)
        PT_full = pst_[0][:, 0:512].bitcast(BF16)
        OT_full = pst_[0][:, 512:1024].bitcast(BF16)
        O_ps = pst_[1]
        def bufs(it):
            r = it % 2
            d = {}
            d["S_ps"] = pst_[2 + r][:, :].rearrange("p (j s) -> p j s", j=4)
            d["tS"] = [t_bank[4 + 2 * r], t_bank[5 + 2 * r]]
            d["sm"] = hv_f32(A_SM + r * 4096, 1024).rearrange("p (j s) -> p j s", j=4)
            d["t_sm"] = hg(A_SM + r * 4096, A_SM + r * 4096 + 4096)
            d["pbv"] = hv_bf(A_PB + r * 2048, 1024).rearrange("p (j s) -> p j s", j=4)
            d["t_pb"] = hg(A_PB + r * 2048, A_PB + r * 2048 + 2048)
            d["ptv"] = hv_bf(A_PT + r * 2048, 1024)
            d["t_pt"] = hg(A_PT + r * 2048, A_PT + r * 2048 + 2048)
            d["sx"] = stt[r]
            d["tsx"] = t_stt[r]
            return d

        def geo(it):
            qb, g = it // 4, it % 4
            return qb, g, qb // 4, g % 2, 4 * (g // 2)

        def st_S(it):
            qb, g, n, half, cbase = geo(it)
            d = bufs(it)
            for j in range(4):
                c = cbase + j
                OP("pe", "matmul", reads=t_qT(c, n) + [t_kTz[g][qb], t_kTz[g][qb + 1]], writes=d["tS"], out=d["S_ps"][:, j, :],
                   lhsT=qT(c, qb * 128, (qb + 1) * 128), rhs=kTz[g][:, qb * 128:qb * 128 + 256], start=True, stop=True)

        def st_D1(it):
            qb, g, n, half, cbase = geo(it)
            d = bufs(it)
            sm, t_sm = d["sm"], d["t_sm"]
            sx, tsx = stt3[it % 3], t_stt3[it % 3]
            mi_ = 1 if (mt == 0 and qb == 0) else 0
            OP("dve", "tensor_tensor", reads=d["tS"] + [t_const], writes=t_sm, out=sm, in0=d["S_ps"],
               in1=msk[:, mi_, :].unsqueeze(1).to_broadcast([128, 4, 256]), op=ALU.add)
            OP("dve", "tensor_reduce", reads=t_sm, writes=[tsx], out=sx[:, 0:4], in_=sm, axis=AX.X, op=ALU.max)
            sinkg = vecs[:, V_SINK + 4 * g:V_SINK + 4 * g + 4]
            OP("dve", "scalar_tensor_tensor", reads=[tsx, t_const], writes=[tsx], out=sx[:, 4:8], in0=sx[:, 0:4], scalar=0.125, in1=sinkg,
               op0=ALU.mult, op1=ALU.max)
            OP("dve", "tensor_scalar", reads=[tsx], writes=[tsx], out=sx[:, 8:12], in0=sx[:, 4:8], scalar1=-1.0, scalar2=None, op0=ALU.mult)
            OP("dve", "tensor_tensor", reads=[tsx, t_const], writes=[tsx], out=sx[:, 16:20], in0=sinkg, in1=sx[:, 4:8], op=ALU.subtract)
            OP("dve", "memset", reads=[tsx], writes=[tsx], ap=sx[:, 12:16], constant=0.0)

        def st_E(it):
            d = bufs(it)
            sm, t_sm, pbv, t_pb = d["sm"], d["t_sm"], d["pbv"], d["t_pb"]
            sx, tsx = stt3[it % 3], t_stt3[it % 3]
            for j in range(4):
                OP("act", "activation", reads=t_sm + [tsx], writes=t_pb + [tsx], out=pbv[:, j, :], in_=sm[:, j, :], func=AF.Exp,
                   bias=sx[:, 8 + j:9 + j], scale=0.125, accum_out=sx[:, 12 + j:13 + j])
            OP("act", "activation", reads=[tsx], writes=[tsx], out=sx[:, 20:24], in_=sx[:, 16:20], func=AF.Exp)

        def st_D2(it):
            sx, tsx = stt3[it % 3], t_stt3[it % 3]
            OP("dve", "tensor_tensor", reads=[tsx], writes=[tsx], out=sx[:, 24:28], in0=sx[:, 12:16], in1=sx[:, 20:24], op=ALU.add)
            OP("dve", "reciprocal", reads=[tsx], writes=[tsx], out=sx[:, 28:32], in_=sx[:, 24:28])

        def st_T(it):
            d = bufs(it)
            for j in range(4):
                for hf in range(2):
                    o_ = (j * 2 + hf) * 128
                    OP("pe", "transpose", reads=d["t_pb"] + [t_const], writes=[t_bank[0]], out=PT_full[:, o_:o_ + 128],
                       in_=d["pbv"][:, j, hf * 128:(hf + 1) * 128], identity=identb[:])

        def st_C(it):
            d = bufs(it)
            OP("act", "copy", reads=[t_bank[0]], writes=d["t_pt"], out=d["ptv"], in_=PT_full)

        def st_PV(it):
            qb, g, n, half, cbase = geo(it)
            d = bufs(it)
            for j in range(4):
                pos = (cbase + j) * 2 + half
                for hf in range(2):
                    o_ = (j * 2 + hf) * 128
                    OP("pe", "matmul", reads=d["t_pt"] + [t_v[qb + hf]], writes=[t_bank[2], t_bank[3]], out=O_ps[:, pos * 64:(pos + 1) * 64],
                       lhsT=d["ptv"][:, o_:o_ + 128], rhs=v_sb[:, qb + hf, g * 64:(g + 1) * 64], start=(hf == 0), stop=(hf == 1))

        def st_D3(it):
            qb, g, n, half, cbase = geo(it)
            ob = qb % 2
            o_sb = hv_bf(A_OS + ob * 2048, 1024)
            t_osb = hg(A_OS + ob * 2048, A_OS + ob * 2048 + 2048)
            sx, tsx = stt3[it % 3], t_stt3[it % 3]
            ov = O_ps[:, :].rearrange("p (c two e) -> p c two e", two=2, e=64)[:, cbase:cbase + 4, half, :]
            osv = o_sb.rearrange("p (c two e) -> p c two e", two=2, e=64)[:, cbase:cbase + 4, half, :]
            OP("dve", "tensor_tensor", reads=[t_bank[2], t_bank[3], tsx], writes=t_osb, out=osv, in0=ov,
               in1=sx[:, 28:32].unsqueeze(2).to_broadcast([128, 4, 64]), op=ALU.mult)
            if g == 3:
                for c in range(DC):
                    OP("pe", "transpose", reads=t_osb + [t_const], writes=[t_bank[1]], out=OT_full[:, c * 128:(c + 1) * 128],
                       in_=o_sb[:, c * 128:(c + 1) * 128], identity=identb[:])
                dst = hv_bf(A_OT, DC * TT).rearrange("p (c t) -> p c t", c=DC)[:, :, qb * 128:(qb + 1) * 128]
                OP("act", "copy", reads=[t_bank[1]], writes=[g_ for c in range(DC) for g_ in t_oT(c, n)], out=dst,
                   in_=OT_full.rearrange("p (c t) -> p c t", c=DC))

        NIT = 32
        for i in range(-1, NIT + 1):
            if 0 <= i + 1 < NIT:
                st_S(i + 1)
                st_D1(i + 1)
            if 0 <= i - 1 < NIT:
                st_C(i - 1)
                st_PV(i - 1)
            if 0 <= i < NIT:
                st_E(i)
                st_T(i)
                st_D2(i)
            if 0 <= i - 1 < NIT:
                st_D3(i - 1)
        for g in range(4):
            OP("pool", "tensor_copy", reads=[t_kTz[g][8]], writes=[t_kTz[g][0]], out=kTz[g][:, 0:128], in_=kTz[g][:, TT:TT + 128])
        OP("pool", "tensor_copy", reads=[t_v[8]], writes=[t_v[0]], out=v_sb[:, 0, :], in_=v_sb[:, 8, :])

        def o_rhs(k, n):
            return oT_ap(k, n * NT, (n + 1) * NT), t_oT(k, n)
        proj_ln("o", o_rhs, None, "mix_g1", "mix_b1")

    if L1:
        for g in range(4):
            OP("pool", "memset", writes=t_kTz[g], ap=kTz[g][:], constant=0.0)
        OP("pool", "memset", writes=t_v, ap=v_sb[:], constant=0.0)
    all_xres = [t_xres[c][n] for c in range(DC) for n in range(NSUB)]
    for mt in range(nmt):
        t0 = mt * TT
        P.dma("sp", ch_x, xres[:], xT.rearrange("(c p) t -> p c t", p=128)[:, :, t0:t0 + TT], writes=all_xres)
        for c in range(DC):
            for n in range(NSUB):
                eng = "pool" if (c + n) % 2 == 0 else "dve"
                OP(eng, "tensor_copy", reads=[t_xres[c][n]], writes=[t_xb[c][n]], out=xb[:, c, ns(n)], in_=xres[:, c, ns(n)])
        for l in layers:
            if l == 0:
                conv_layer(mt)
            else:
                attn_layer(mt, t0)
            mlp_ple(l, t0, l == last_layer)
        for n in range(NSUB):
            P.dma("sp", ch_out[n], outT.rearrange("(c p) t -> p c t", p=128)[:, :, t0 + n * NT:t0 + (n + 1) * NT],
                  xres[:, :, ns(n)], reads=[t_xres[c][n] for c in range(DC)], writes=[Tile("o")])
    assert st["pos"] == len(order), (st["pos"], len(order))
    P.finalize(final_waits=ch_out)
    P.close()
    return nc


def _fm(v):
    return np.ascontiguousarray(np.asarray(v, np.float32).reshape(8, 128).T)


def _rope_tables(tlen):
    pos = np.arange(tlen, dtype=np.float32)
    inv_freq = (np.float32(ROPE_THETA) ** (-np.arange(0, 16, 2, dtype=np.float32) / np.float32(16))).astype(np.float32)
    ang = (pos[:, None] * inv_freq[None, :]).astype(np.float32)
    cos, sin = np.cos(ang).astype(np.float32), np.sin(ang).astype(np.float32)
    ct = np.ones((128, tlen), np.float32)
    stb = np.zeros((128, tlen), np.float32)
    for p in range(128):
        d = p % 64
        if d < 8:
            ct[p] = cos[:, d]
            stb[p] = -sin[:, d]
        elif d < 16:
            ct[p] = cos[:, d - 8]
            stb[p] = sin[:, d - 8]
    return np.stack([ct, stb])


def _rot_cols(w, nheads):
    idx = np.arange(nheads * 64).reshape(nheads, 64).copy()
    src = idx.copy()
    src[:, 0:8] = idx[:, 8:16]
    src[:, 8:16] = idx[:, 0:8]
    return w[:, src.reshape(-1)]


def _prep(inp, layers=(0, 1)):
    f = lambda a: np.ascontiguousarray(np.asarray(a, dtype=np.float32))
    shared = {}
    if 0 in layers:
        shared["w_in"] = f(inp["conv_w_in"][0])
        shared["w_cout"] = f(inp["conv_w_out"][0])
    if 1 in layers:
        wk, wv, wq, wo = f(inp["kv_w_k"]), f(inp["kv_w_v"]), f(inp["attn_w_q"][0]), f(inp["attn_w_o"][0])
        shared["w_kk"] = np.ascontiguousarray(np.stack([wk, _rot_cols(wk, 4), wv]))
        colperm = np.concatenate([np.arange(h * 64, (h + 1) * 64) for h in PERM_HEADS])
        shared["w_qq"] = np.ascontiguousarray(np.stack([wq[:, colperm], _rot_cols(wq, 16)[:, colperm]]))
        shared["w_o"] = np.ascontiguousarray(wo[colperm, :])
        a = np.arange(128)[:, None]
        s = np.arange(256)[None, :]
        ok = (s > a) & (s <= a + 128)
        mg = np.where(ok, 0.0, NEG).astype(np.float32)
        mf = np.where(ok & (s >= 128), 0.0, NEG).astype(np.float32)
        shared["msk"] = np.ascontiguousarray(np.stack([mg, mf]))
    shared["w_up"] = f(inp["mlp_w_up"])
    shared["w_down"] = f(inp["mlp_w_down"])
    shared["w_proj"] = f(inp["ple_w_proj"])
    shared["w_gate"] = f(inp["ple_w_gate"])
    vec = np.zeros((128, NV), np.float32)

    def put(name, v):
        vec[:, VOFF[name]:VOFF[name] + 8] = _fm(v)
    b_in = f(inp["conv_b_in"][0])
    put("b_in_a", b_in[:1024]); put("b_in_g", b_in[1024:])
    put("b_dw", inp["conv_b_dw"][0]); put("cln_g", inp["conv_ln_g"][0]); put("cln_b", inp["conv_ln_b"][0])
    put("b_out", inp["conv_b_out"][0])
    for l in range(2):
        put(f"mix_g{l}", inp["mix_ln_g"][l]); put(f"mix_b{l}", inp["mix_ln_b"][l])
        put(f"mlp_g{l}", inp["mlp_ln_g"][l]); put(f"mlp_b{l}", inp["mlp_ln_b"][l])
    wdw = f(inp["conv_w_dw"][0])
    vec[:, V_WDW:V_WDW + 248] = wdw.T.reshape(8, 128, 31).transpose(1, 0, 2).reshape(128, 248)
    vec[:, V_SINK:V_SINK + 16] = np.broadcast_to(f(inp["attn_sinks"][0])[None, :], (128, 16))
    vec[:, V_EPS] = EPS
    shared["vecs"] = vec
    shared["ident"] = np.eye(128, dtype=np.float32)
    return shared


def _run(inp_x, inp_p, shared, layers, nmt=4, ncores=8):
    tlen = nmt * TT
    nc = build(layers=layers, nmt=nmt, tlen=tlen)
    if 1 in layers:
        shared = dict(shared)
        shared["cs"] = _rope_tables(tlen)
    in_maps = []
    for b in range(ncores):
        m = dict(shared)
        m["xT"] = np.ascontiguousarray(inp_x[b][:tlen].T)
        m["pT"] = np.ascontiguousarray(np.transpose(inp_p[:, b, :tlen, :], (0, 2, 1)))
        in_maps.append(m)
    res = run_bass_kernel_spmd(nc, in_maps, core_ids=list(range(ncores)))
    return [np.ascontiguousarray(r["outT"].T) for r in res.results]


def kernel(**inputs):
    x = np.asarray(inputs["x"], np.float32)
    p = np.asarray(inputs["p"], np.float32)
    shared = _prep(inputs, (0, 1))
    outs = _run(x, p, shared, (0, 1), nmt=4, ncores=8)
    return np.stack(outs).astype(np.float32)
```
